# Optimizing a Trainium2 kernel written in Bass

```python
import jax, jax.numpy as jnp
from jax import lax
import numpy as np

D_MODEL = 1024
BATCH = 4
SEQ = 4096
DEPTH = 4
DEC_BATCH = 16
DEC_SEQ = 16
PAST_LEN = 1024

CHUNK = 64
D_MIX = D_MODEL
POOL_WIDTH = D_MIX // 2
POOL_WINDOWS = (2, 4, 8, 16)
N_POOL_GROUPS = len(POOL_WINDOWS)
POOL_GROUP = POOL_WIDTH // N_POOL_GROUPS
POOL_BUF = max(POOL_WINDOWS) - 1
GMLP_WIDTH = D_MIX - POOL_WIDTH
GMLP_HEADS = 4
GMLP_HEAD = GMLP_WIDTH // GMLP_HEADS
GMLP_LEN = 128
D_IN = POOL_WIDTH + 2 * GMLP_WIDTH
D_FF = ((8 * D_MODEL // 3 + 255) // 256) * 256
N_MOD = 6
EPS = 1e-6

kernel_name = "pool_gmlp_hybrid_stream_step"


def rms_norm(x, g):
    xf = x.astype(jnp.float32)
    y = xf * lax.rsqrt(jnp.mean(xf * xf, axis=-1, keepdims=True) + EPS)
    return (y * g.astype(jnp.float32)).astype(x.dtype)


def layer_norm(x, g, b):
    xf = x.astype(jnp.float32)
    mu = jnp.mean(xf, axis=-1, keepdims=True)
    var = jnp.mean(jnp.square(xf - mu), axis=-1, keepdims=True)
    y = (xf - mu) * lax.rsqrt(var + EPS)
    return (y * g.astype(jnp.float32) + b.astype(jnp.float32)).astype(x.dtype)


def pool_mixer(p, hist, pos0, w_pool, pool_scale):
    B, L, _ = p.shape
    full = jnp.concatenate([hist.astype(p.dtype), p], axis=1)
    cs = jnp.cumsum(full.astype(jnp.float32), axis=1)
    cs = jnp.pad(cs, ((0, 0), (1, 0), (0, 0)))
    end = POOL_BUF + 1
    pos = pos0 + jnp.arange(L, dtype=jnp.int32)
    outs = []
    for g, w in enumerate(POOL_WINDOWS):
        sl = slice(g * POOL_GROUP, (g + 1) * POOL_GROUP)
        win_sum = cs[:, end:end + L, sl] - cs[:, end - w:end - w + L, sl]
        count = jnp.minimum(pos + 1, w).astype(jnp.float32)[None, :, None]
        outs.append(win_sum / count - p[:, :, sl].astype(jnp.float32))
    d = jnp.stack(outs, axis=2).astype(p.dtype)
    y = jnp.einsum('blgc,gcd->blgd', d, w_pool).reshape(B, L, POOL_WIDTH) * pool_scale
    return y, full[:, -POOL_BUF:]


def gmlp_mixer(u, v, w_s, b_s):
    B, L, _ = v.shape
    Lc = min(L, GMLP_LEN)
    N = L // Lc
    idx = jnp.arange(GMLP_LEN) // CHUNK
    mask = idx[None, :] <= idx[:, None]
    wm = jnp.where(mask[None], w_s, jnp.zeros((), w_s.dtype))[:, :Lc, :Lc]
    vb = v.reshape(B, N, Lc, GMLP_HEADS, GMLP_HEAD)
    z = jnp.einsum('hts,bnshc->bnthc', wm.astype(v.dtype), vb)
    z = z + b_s[:, :Lc].T.astype(v.dtype)[None, None, :, :, None]
    return u * z.reshape(B, L, GMLP_WIDTH)


def trunk_layer(x, c, hist, pos0, w_ada, b_ada, g_mix, w_in, w_pool, pool_scale, ln_g, ln_b,
                w_s, b_s, w_out, g_ffn, w_gate, w_up, w_down):
    mod = jnp.dot(jax.nn.silu(c), w_ada) + b_ada
    sh_m, sc_m, gt_m, sh_f, sc_f, gt_f = [m[:, None, :] for m in jnp.split(mod, N_MOD, axis=-1)]
    h = rms_norm(x, g_mix) * (1 + sc_m) + sh_m
    z = jnp.dot(h, w_in)
    p = z[..., :POOL_WIDTH]
    u = jax.nn.gelu(z[..., POOL_WIDTH:POOL_WIDTH + GMLP_WIDTH])
    v = layer_norm(jax.nn.gelu(z[..., POOL_WIDTH + GMLP_WIDTH:]), ln_g, ln_b)
    y_pool, new_hist = pool_mixer(p, hist, pos0, w_pool, pool_scale)
    y_gmlp = gmlp_mixer(u, v, w_s, b_s)
    mix = jnp.dot(jnp.concatenate([y_pool, y_gmlp], axis=-1), w_out)
    x = x + gt_m * mix
    h = rms_norm(x, g_ffn) * (1 + sc_f) + sh_f
    f = jnp.dot(jax.nn.silu(jnp.dot(h, w_gate)) * jnp.dot(h, w_up), w_down)
    x = x + gt_f * f
    return x, new_hist, v


def setup_inputs(seed: int = 0) -> dict:
    key = jax.random.key(seed)
    ks = jax.random.split(key, 24)
    f32 = jnp.float32
    nrm = lambda k, s, sc: jax.random.normal(k, s, f32) * sc
    return {
        "x_prompt": nrm(ks[0], (BATCH, SEQ, D_MODEL), 1.0),
        "x_sample": nrm(ks[1], (DEC_BATCH, DEC_SEQ, D_MODEL), 1.0),
        "c_prompt": nrm(ks[2], (BATCH, D_MODEL), 1.0),
        "c_sample": nrm(ks[3], (DEC_BATCH, D_MODEL), 1.0),
        "cache_pool": nrm(ks[4], (DEPTH, DEC_BATCH, POOL_BUF, POOL_WIDTH), 1.0),
        "w_ada": nrm(ks[5], (DEPTH, D_MODEL, N_MOD * D_MODEL), 0.5 * D_MODEL ** -0.5),
        "b_ada": nrm(ks[6], (DEPTH, N_MOD * D_MODEL), 0.01),
        "g_mix": 1.0 + nrm(ks[7], (DEPTH, D_MODEL), 0.02),
        "w_in": nrm(ks[8], (DEPTH, D_MODEL, D_IN), D_MODEL ** -0.5),
        "w_pool": nrm(ks[9], (DEPTH, N_POOL_GROUPS, POOL_GROUP, POOL_GROUP), POOL_GROUP ** -0.5),
        "pool_scale": 1.0 + nrm(ks[10], (DEPTH, POOL_WIDTH), 0.02),
        "ln_g": 1.0 + nrm(ks[11], (DEPTH, GMLP_WIDTH), 0.02),
        "ln_b": nrm(ks[12], (DEPTH, GMLP_WIDTH), 0.02),
        "w_s": nrm(ks[13], (DEPTH, GMLP_HEADS, GMLP_LEN, GMLP_LEN), GMLP_LEN ** -0.5),
        "b_s": 1.0 + nrm(ks[14], (DEPTH, GMLP_HEADS, GMLP_LEN), 0.02),
        "w_out": nrm(ks[15], (DEPTH, D_MIX, D_MODEL), D_MIX ** -0.5),
        "g_ffn": 1.0 + nrm(ks[16], (DEPTH, D_MODEL), 0.02),
        "w_gate": nrm(ks[17], (DEPTH, D_MODEL, D_FF), D_MODEL ** -0.5),
        "w_up": nrm(ks[18], (DEPTH, D_MODEL, D_FF), D_MODEL ** -0.5),
        "w_down": nrm(ks[19], (DEPTH, D_FF, D_MODEL), D_FF ** -0.5),
        "g_final": 1.0 + nrm(ks[20], (D_MODEL,), 0.02),
    }


def reference(x_prompt, x_sample, c_prompt, c_sample, cache_pool, w_ada, b_ada, g_mix, w_in,
              w_pool, pool_scale, ln_g, ln_b, w_s, b_s, w_out, g_ffn, w_gate, w_up, w_down, g_final):
    xp, xs = x_prompt, x_sample
    hist_p0 = jnp.zeros((x_prompt.shape[0], POOL_BUF, POOL_WIDTH), x_prompt.dtype)
    pool_p, pool_s, v_s = [], [], []
    for l in range(DEPTH):
        lw = (w_ada[l], b_ada[l], g_mix[l], w_in[l], w_pool[l], pool_scale[l], ln_g[l], ln_b[l],
              w_s[l], b_s[l], w_out[l], g_ffn[l], w_gate[l], w_up[l], w_down[l])
        xp, hp, _ = trunk_layer(xp, c_prompt, hist_p0, 0, *lw)
        xs, hs, vs = trunk_layer(xs, c_sample, cache_pool[l], PAST_LEN, *lw)
        pool_p.append(hp)
        pool_s.append(hs)
        v_s.append(vs)
    y_prompt = rms_norm(xp, g_final)
    y_sample = rms_norm(xs, g_final)
    state_pool_prompt = jnp.stack(pool_p)
    state_pool_sample = jnp.stack(pool_s)
    state_gmlp_v_sample = jnp.stack(v_s)
    return (y_prompt, y_sample, state_pool_prompt, state_pool_sample, state_gmlp_v_sample)
```

```python
import numpy as np
from contextlib import ExitStack
import concourse.bass as bass
import concourse.mybir as mybir
from concourse.bass_utils import run_bass_kernel_spmd

F32, BF16 = mybir.dt.float32, mybir.dt.bfloat16
AF = mybir.ActivationFunctionType
ALU = mybir.AluOpType

D = 1024
KC = 8
DEPTH = 4
DFF = 2816
NFC = 22
NBLK = 20
BT = 128
NSMP = 32
SBLK = NBLK
TT = NBLK * BT + NSMP
HT = (NBLK - 1) * BT + NSMP
EPS = 1e-6
POOL_W = (2, 4, 8, 16)
NRING = 6


class Tok:
    __slots__ = ("sem", "val")

    def __init__(self, sem, val):
        self.sem, self.val = sem, val


class Res:
    __slots__ = ("name", "w", "r")

    def __init__(self, name):
        self.name, self.w, self.r = name, None, []


class DmaSem:
    def __init__(self, sem):
        self.sem, self.count = sem, 0


class EngS:
    def __init__(self, name, sem):
        self.name, self.sem = name, sem
        self.count = 0
        self.ops = []
        self.known = {}
        self.pending = []


class Sched:
    def __init__(self):
        self.eng = {}

    def add_engine(self, name, sem):
        self.eng[name] = EngS(name, sem)

    def op(self, eng, emit, reads=(), writes=(), signal=True, dma=None):
        E = self.eng[eng]
        dsem_obj = dma.sem if dma is not None else None
        deps = []
        for r in reads:
            if r.w is not None:
                deps.append(r.w)
        for w in writes:
            if w.w is not None:
                deps.append(w.w)
            deps.extend(w.r)
        waits = {}
        for t in deps:
            if eng == "pe" and t.sem is E.sem:
                continue
            assert t.val is not None, "dependency on an unsignaled op"
            if dsem_obj is not None and t.sem is dsem_obj:
                continue
            if E.known.get(id(t.sem), 0) >= t.val:
                continue
            k = id(t.sem)
            if k not in waits or waits[k][1] < t.val:
                waits[k] = (t.sem, t.val)
        for k, (s, v) in waits.items():
            E.known[k] = v
        if dma is not None:
            dma.count += 16
            tok = Tok(dma.sem, dma.count)
            inc = (dma.sem, 16)
        elif signal:
            E.count += 1
            tok = Tok(E.sem, E.count)
            inc = (E.sem, 1)
            for p in E.pending:
                p.val = E.count
            E.pending = []
        else:
            tok = Tok(E.sem, None)
            E.pending.append(tok)
            inc = None
        E.ops.append((list(waits.values()), emit, inc))
        for r in reads:
            r.r.append(tok)
        for w in writes:
            w.w = tok
            w.r = []
        return tok

    def barrier(self):
        snap = {n: (E.sem, E.count) for n, E in self.eng.items()}
        for n, E in self.eng.items():
            assert not E.pending
            waits = []
            for m, (s, v) in snap.items():
                if m != n and v > 0 and E.known.get(id(s), 0) < v:
                    waits.append((s, v))
                    E.known[id(s)] = v
            E.ops.append((waits, None, None))

    def replay(self, name, handle):
        E = self.eng[name]
        for waits, emit, inc in E.ops:
            for (s, v) in waits:
                handle.wait_ge(s, v)
            if emit is None:
                continue
            inst = emit(handle)
            if inc is not None:
                inst.then_inc(inc[0], inc[1])


def tiles_of(blocks, per):
    full = [b for b in blocks if b != SBLK]
    groups = [full[i:i + per] for i in range(0, len(full), per)]
    if SBLK in blocks:
        if groups and len(groups[-1]) < per:
            groups[-1] = groups[-1] + [SBLK]
        else:
            groups.append([SBLK])
    return groups


def bcols(b):
    if b == SBLK:
        return NBLK * BT, NSMP
    return b * BT, BT


def build(n_layers=DEPTH, do_final=True, wdepth=DEPTH):
    nc = bass.Bass("TRN2", target_bir_lowering=False)

    def din(name, shape):
        return nc.dram_tensor(name, list(shape), F32, kind="ExternalInput").ap()

    def dout(name, shape):
        return nc.dram_tensor(name, list(shape), F32, kind="ExternalOutput").ap()

    xin = din("xin", [TT, D])
    c3 = din("c3", [3, D])
    cache = din("cache", [wdepth, 2, 15, 512])
    band_in = {n: din(n, [128, 4, 128]) for n in ("bm", "b4m")}
    bandp_in = {n: din(n, [128, 4, 16]) for n in ("bp", "b4p")}
    bms_in = din("bms", [32, 4, 32])
    bps_in = din("bps", [32, 4, 32])
    ident_in = din("ident", [128, 128])
    w_ada = din("w_ada", [wdepth, D, 6 * D])
    b_ada = din("b_ada", [wdepth, 6 * D])
    g_mix = din("g_mix", [wdepth, D])
    w_in = din("w_in", [wdepth, D, 1536])
    w_pool = din("w_pool", [wdepth, 4, 128, 128])
    pool_scale = din("pool_scale", [wdepth, 512])
    ln_g = din("ln_g", [wdepth, 512])
    ln_b = din("ln_b", [wdepth, 512])
    w_s = din("w_s", [wdepth, 4, 128, 128])
    b_s = din("b_s", [wdepth, 4, 128])
    w_out = din("w_out", [wdepth, D, D])
    g_ffn = din("g_ffn", [wdepth, D])
    w_gate = din("w_gate", [wdepth, D, DFF])
    w_up = din("w_up", [wdepth, D, DFF])
    w_down = din("w_down", [wdepth, DFF, D])
    g_final = din("g_final", [D])

    y_out = dout("y_out", [16 * BT + NSMP, D])
    sp_p = dout("sp_p", [DEPTH, 15, 512])
    sp_s = dout("sp_s", [DEPTH, 2, 15, 512])
    sv_s = dout("sv_s", [DEPTH, 2, 16, 512])

    es = ExitStack()
    with es:
        def sb(name, shape, dt):
            return es.enter_context(nc.sbuf_tensor(name, list(shape), dt))

        x_sb = sb("x_sb", [128, KC, TT], F32)
        ring = [sb(f"ring{i}", [128, 4096], BF16) for i in range(NRING)]
        hreg = sb("hreg", [128, KC * HT], BF16)
        wada = [sb(f"wada{i}", [128, KC, 128], BF16) for i in range(2)]
        brow = [sb(f"brow{i}", [3, 128], F32) for i in range(2)]
        a_sb = [sb(f"a{i}", [128, 4, 512], BF16) for i in range(2)]
        sg_sb = [sb(f"sg{i}", [128, 512], BF16) for i in range(2)]
        identf = sb("identf", [128, 128], F32)
        onesf = sb("onesf", [128, 128], F32)
        onesb = sb("onesb", [128, 128], BF16)
        bands = {n: sb("band_" + n, [128, 4, 128], BF16) for n in ("bm", "b4m")}
        bandsp = {n: sb("band_" + n, [128, 4, 16], BF16) for n in ("bp", "b4p")}
        bms = sb("bms_sb", [32, 4, 32], BF16)
        bps = sb("bps_sb", [32, 4, 32], BF16)
        modT = sb("modT", [128, DEPTH, 48, 3], F32)
        gsc = sb("gsc", [128, DEPTH, 2, KC, 3], F32)
        gT = sb("gT", [128, 2, DEPTH, KC], F32)
        cT = sb("cT", [128, KC, 3], F32)
        scT = sb("scT", [128, KC, 3], BF16)
        modrow = [sb(f"modrow{i}", [3, 128], F32) for i in range(2)]
        psT = sb("psT", [128, DEPTH, 4], F32)
        wpool_sb = sb("wpool", [128, 4, 128], BF16)
        wmT = sb("wmT", [128, 4, 128], BF16)
        wmT_s = sb("wmT_s", [32, 4, 32], BF16)
        rows = sb("rows", [1, 4 * 128 + 4 * 32], BF16)
        lng_bc = sb("lng_bc", [128, 512], F32)
        lnb_bc = sb("lnb_bc", [128, 512], F32)
        pprev_s = sb("pprev_s", [32, 512], BF16)
        rstd_tok = sb("rstd_tok", [128, 24], F32)
        srt_tok = sb("srt_tok", [128, 24], F32)
        bnst = sb("bnst", [128, 4, 6], F32)
        mv = sb("mv", [128, 4, 2], F32)
        rv = sb("rv", [128, 4], F32)
        sv = sb("sv", [128, 4], F32)
        diagb = [sb(f"diagb{i}", [128, 2, 128], BF16) for i in range(2)]
        diagb_s = sb("diagb_s", [32, 2, 32], BF16)
        eps_t = sb("eps_t", [128, 1], F32)

        hoff = [0]

        def hview(nelem_bf16):
            o = hoff[0]
            hoff[0] += nelem_bf16
            assert hoff[0] <= KC * HT, "hreg overflow %d" % hoff[0]
            return hreg[:, o:o + nelem_bf16]

        NM = 288
        h_m = [hview(KC * NM).rearrange("p (k t) -> p k t", k=KC) for _ in range(2)]
        u_m = [hview(4 * NM).rearrange("p (k t) -> p k t", k=4) for _ in range(2)]
        ycat = hview(KC * NM).rearrange("p (k t) -> p k t", k=KC)
        d_m = hview(4 * NM).rearrange("p (k t) -> p k t", k=4)
        NPT = 5
        ptok = [hview(512) for _ in range(NPT)]
        ptok_s = hview(512)
        vtok = [hview(512) for _ in range(4)]
        vtok_s = hview(512)
        g_m = [hview(2 * 512).bitcast(F32) for _ in range(2)]
        def h_tile(ti, N):
            return hreg[:, ti * 4096:ti * 4096 + KC * N].rearrange("p (k t) -> p k t", k=KC)

        def seed(dst, srcs):
            toks = []
            for s in srcs:
                if s.w is not None:
                    toks.append(s.w)
                toks.extend(s.r)
            for d in dst:
                d.r.extend(toks)
        tmp_m = [sg_sb[i][:, :].bitcast(F32) for i in range(2)]
        tmp_f = [a_sb[i][:, :, :].rearrange("p a b -> p (a b)").bitcast(F32) for i in range(2)]
        stage = tmp_f
        pf32 = tmp_f[1][:, 0:512]
        vf32 = tmp_f[1][:, 512:1024]
        wsf = g_m[1].rearrange("p (h s) -> p h s", h=4)

        banks = [es.enter_context(nc.psum_tensor(f"bank{i}", [128, 512], F32)) for i in range(8)]

        def sem(name):
            return es.enter_context(nc.semaphore(name))

        S = Sched()
        for e in ("pe", "act", "dve", "sp", "pool"):
            S.add_engine(e, sem("s_" + e))

        def dsem(name):
            return DmaSem(sem("d_" + name))

        Rx = [[Res(f"x{k}_{b}") for b in range(NBLK + 1)] for k in range(KC)]
        Rbank = [Res(f"bank{i}") for i in range(8)]
        Rring = [Res(f"ring{i}") for i in range(NRING)]
        Dring = [dsem(f"ring{i}") for i in range(NRING)]
        Rwada = [Res("wada0"), Res("wada1")]
        Dwada = [dsem("wada0"), dsem("wada1")]
        Rbrow = [Res("brow0"), Res("brow1")]
        Dbrow = [dsem("brow0"), dsem("brow1")]
        Rmodrow = [Res("modrow0"), Res("modrow1")]
        Rconst = Res("const")
        Dconst = dsem("const")
        Rconstp = Res("constp")
        Dconstp = dsem("constp")
        Rln = Res("ln")
        Dln = dsem("ln")
        RmodT = [Res(f"modT{l}") for l in range(DEPTH)]
        Rgsc = [Res(f"gsc{l}") for l in range(DEPTH)]
        Rsmall = Res("small")
        Dsmall = dsem("small")
        Dwsf = dsem("wsf")
        Dwms = dsem("wms")
        Rwm = Res("wmT")
        Rwms = Res("wmT_s")
        Rpprev = Res("pprev_s")
        Dpprev = dsem("pprev")
        Rrstd = Res("rstd_tok")
        Rsrt = Res("srt_tok")
        Rdiag = [Res("diag0"), Res("diag1"), Res("diag_s")]
        Rxsq = [Res("xsq0")]
        Rh_m = [[Res(f"hm{i}_{k}") for k in range(KC)] for i in range(2)]
        Ru_m = [Res("um0"), Res("um1")]
        Rycat = Res("ycat")
        Rd_m = Res("dm")
        Rptok = [Res(f"ptok{i}") for i in range(NPT)]
        Rptok_s = Res("ptok_s")
        Rvtok = [Res(f"vtok{i}") for i in range(4)]
        Rvtok_s = Res("vtok_s")
        Rg_m = [Res("gm0"), Res("gm1")]
        Rbn = Res("bnst")
        Rmv = Res("mv")
        Rrv = Res("rv")
        Ra = [Res("a0"), Res("a1")]
        Da = [dsem("a0"), dsem("a1")]
        Rsg = [Res("sg0"), Res("sg1")]
        Dgfin = dsem("gfin")
        Rdram_out = Res("dram_out")
        Ronesf, Ronesb, Reps = Res("onesf"), Res("onesb"), Res("eps")
        out_toks = []

        bank_rr = [0]

        def alloc_bank():
            i = bank_rr[0]
            bank_rr[0] = (i + 1) % 6
            return i

        ring_rr = [0]

        def alloc_ring():
            i = ring_rr[0]
            ring_rr[0] = (i + 1) % NRING
            return i

        def mm(out, lhsT, rhs, start, stop, reads, writes, signal):
            return S.op("pe", lambda e: e.matmul(out, lhsT, rhs, start=start, stop=stop),
                        reads=reads, writes=writes, signal=signal)

        def act(out, in_, func, reads, writes, bias=None, scale=None):
            kw = {}
            if bias is not None:
                kw["bias"] = bias
            if scale is not None:
                kw["scale"] = scale
            return S.op("act", lambda e: e.activation(out=out, in_=in_, func=func, **kw),
                        reads=reads, writes=writes)

        def dve(fn, reads, writes):
            return S.op("dve", fn, reads=reads, writes=writes)

        def dma(q, out, in_, reads, writes, dsem_, nonc=False):
            if nonc:
                return S.op(q, lambda e: e.dma_start(out=out, in_=in_, allow_slow_non_contiguous=True),
                            reads=reads, writes=writes, dma=dsem_)
            return S.op(q, lambda e: e.dma_start(out=out, in_=in_), reads=reads, writes=writes, dma=dsem_)

        rr_evac = [0]

        def copy_any(out, in_, reads, writes):
            rr_evac[0] ^= 1
            if rr_evac[0]:
                return act(out, in_, AF.Identity, reads, writes)
            return dve(lambda e: e.tensor_copy(out=out, in_=in_), reads, writes)

        dma("sp", identf[:, :], ident_in[:, :], [], [Rconst], Dconst)
        for n in ("bm", "b4m"):
            dma("pool", bands[n][:, :, :], band_in[n][:, :, :], [], [Rconstp], Dconstp)
        for n in ("bp", "b4p"):
            dma("pool", bandsp[n][:, :, :], bandp_in[n][:, :, :], [], [Rconstp], Dconstp)
        dma("pool", bms[:, :, :], bms_in[:, :, :], [], [Rconstp], Dconstp)
        dma("pool", bps[:, :, :], bps_in[:, :, :], [], [Rconstp], Dconstp)
        for s in range(3):
            dma("sp", cT[:, :, s], c3[s].rearrange("(k p) -> p k", p=128), [], [Rconst], Dconst, nonc=True)
        Rconst2 = Res("const2")
        Dconst2 = dsem("const2")

        gain_loads = []
        for l in range(wdepth):
            gain_loads.append(lambda l=l: dma("sp", gT[:, 0, l, :], g_mix[l].rearrange("(k p) -> p k", p=128),
                                              [], [Rconst2], Dconst2, nonc=True))
            gain_loads.append(lambda l=l: dma("sp", gT[:, 1, l, :], g_ffn[l].rearrange("(k p) -> p k", p=128),
                                              [], [Rconst2], Dconst2, nonc=True))
            gain_loads.append(lambda l=l: dma("sp", psT[:, l, :], pool_scale[l].rearrange("(g p) -> p g", p=128),
                                              [], [Rconst2], Dconst2, nonc=True))
        dve(lambda e: e.memset(onesf[:, :], 1.0), [], [Ronesf])
        dve(lambda e: e.memset(onesb[:, :], 1.0), [], [Ronesb])
        dve(lambda e: e.memset(eps_t[:, :], EPS), [], [Reps])
        dve(lambda e: e.memset(pprev_s[:, :], 0.0), [], [Rpprev])
        dve(lambda e: e.memset(banks[6][:, 0:32], 0.0), [], [Rbank[6]])
        dve(lambda e: e.memset(wmT_s[:, :, :], 0.0), [], [Rwms])
        act(scT[:, :, :], cT[:, :, :], AF.Silu, [Rconst], [Rconst])

        wada_rr = [0]
        mod_pending = []

        def mod_flush():
            while mod_pending:
                mod_pending.pop(0)()

        def mod_piece(l, j, src=None):
            i = wada_rr[0]
            wada_rr[0] ^= 1
            if src is None:
                dma("pool", wada[i][:, :, :],
                    w_ada[l, :, j * 128:(j + 1) * 128].rearrange("(k p) c -> p k c", p=128),
                    [], [Rwada[i]], Dwada[i])
                wview, Rw = wada[i], Rwada[i]
            else:
                ri, rview = src
                wview, Rw = rview[:, :, (j % 4) * 128:(j % 4 + 1) * 128], Rring[ri]
            dma("sp", brow[i][:, :], b_ada[l, j * 128:(j + 1) * 128].partition_broadcast(3),
                [], [Rbrow[i]], Dbrow[i])
            bk = alloc_bank()
            for k in range(KC):
                mm(banks[bk][0:3, 0:128], scT[:, k, 0:3], wview[:, k, :], k == 0, k == KC - 1,
                   [Rconst, Rw], [Rbank[bk]], k == KC - 1)
            mod_flush()
            dve(lambda e, i=i, bk=bk: e.tensor_tensor(out=modrow[i][:, :], in0=banks[bk][0:3, 0:128],
                                                       in1=brow[i][:, :], op=ALU.add),
                [Rbank[bk], Rbrow[i]], [Rmodrow[i]])

            def fin(l=l, j=j, i=i):
                bk2 = alloc_bank()
                S.op("pe", lambda e: e.transpose(out=banks[bk2][:, 0:3], in_=modrow[i][0:3, :],
                                                 identity=identf[0:3, 0:3]),
                     reads=[Rmodrow[i], Rconst], writes=[Rbank[bk2]])
                dve(lambda e: e.tensor_copy(out=modT[:, l, j, :], in_=banks[bk2][:, 0:3]),
                    [Rbank[bk2]], [RmodT[l]])
            mod_pending.append(fin)

        mod_queue = []

        def mod_work(n):
            for _ in range(n):
                if mod_queue:
                    l, j, wsrc = mod_queue.pop(0)
                    mod_piece(l, j, wsrc)
                    if j == 23:
                        mod_flush()
                        mod_finish(l, 0)
                    if j == 47:
                        mod_flush()
                        mod_finish(l, 1)

        def mod_finish(l, which):
            base = 8 if which == 0 else 32
            for s in range(3):
                dve(lambda e, s=s: e.scalar_tensor_tensor(
                    out=gsc[:, l, which, :, s], in0=modT[:, l, base:base + KC, s], scalar=1.0,
                    in1=gT[:, which, l, :], op0=ALU.add, op1=ALU.mult),
                    [RmodT[l], Rconst2], [Rgsc[l]])

        def load_piece(src_ap, a, b):
            i = alloc_ring()
            dst = ring[i][:, 0:a * b].rearrange("p (a b) -> p a b", a=a)
            dma("pool", dst, src_ap, [], [Rring[i]], Dring[i])
            return i, dst

        def load_cols(w, l, c0, ncol):
            return load_piece(w[l, :, c0:c0 + ncol].rearrange("(k p) c -> p k c", p=128), KC, ncol)

        def load_rows(w, l, r0, nch):
            return load_piece(w[l, r0:r0 + nch * 128, :].rearrange("(c p) n -> p c n", p=128), nch, D)

        def load_small(l):
            dma("pool", wpool_sb[:, :, :], w_pool[l].rearrange("g c d -> c g d"), [], [Rsmall], Dsmall)
            dma("pool", rows[0:1, 0:512], b_s[l:l + 1].rearrange("o h t -> o (h t)"), [], [Rsmall], Dsmall)
            for j in range(2):
                dma("pool", rows[0:1, 512:640].rearrange("o (h t) -> o h t", h=4)[:, :, 16 * j:16 * j + 16],
                    b_s[l:l + 1, :, 0:16], [], [Rsmall], Dsmall, nonc=True)
            dma("sp", lng_bc[:, :], ln_g[l].partition_broadcast(128), [], [Rln], Dln)
            dma("sp", lnb_bc[:, :], ln_b[l].partition_broadcast(128), [], [Rln], Dln)
            for j in range(2):
                dma("pool", pprev_s[16 * j:16 * j + 15, :], cache[l, j], [], [Rpprev], Dpprev)

        def prep_wm(l):
            dma("sp", wsf[:, :, :], w_s[l].rearrange("h t s -> t h s"), [], [Rg_m[1]], Dwsf)
            bk = alloc_bank()
            for h in range(4):
                S.op("pe", lambda e, h=h, bk=bk: e.transpose(out=banks[bk][:, h * 128:(h + 1) * 128],
                                                             in_=wsf[:, h, :], identity=identf[:, :]),
                     reads=[Rg_m[1], Rconst], writes=[Rbank[bk]], signal=(h == 3))
            dve(lambda e, bk=bk: e.tensor_copy(out=wmT[:, :, :],
                                                in_=banks[bk][:, :].rearrange("p (h t) -> p h t", h=4)),
                [Rbank[bk]], [Rwm])
            dve(lambda e: e.memset(wmT[64:128, :, 0:64], 0.0), [], [Rwm])
            for j in range(2):
                for h in range(4):
                    dma("pool", wmT_s[16 * j:16 * j + 16, h, 16 * j:16 * j + 16],
                        w_s[l, h, 0:16, 0:16].rearrange("t s -> s t"), [], [Rwms], Dwms, nonc=True)

        SBK = 6

        def stats_block(b, xq, Rxq):
            c0, nb = bcols(b)
            act(xq[:, :, 0:nb], x_sb[:, :, c0:c0 + nb], AF.Square,
                [Rx[k][b] for k in range(KC)], [Rxq])
            for k in range(KC):
                mm(banks[SBK][0:nb, b:b + 1], xq[:, k, 0:nb], onesb[:, 0:1], k == 0, k == KC - 1,
                   [Rxq, Ronesb], [Rbank[SBK]], k == KC - 1)

        def stats_finish(c0=0, c1=NBLK + 1):
            act(srt_tok[:, c0:c1], banks[SBK][:, c0:c1], AF.Sqrt, [Rbank[SBK], Reps], [Rsrt],
                bias=eps_t[:, 0:1], scale=1.0 / D)
            dve(lambda e: e.reciprocal(out=rstd_tok[:, c0:c1], in_=srt_tok[:, c0:c1]),
                [Rsrt], [Rrstd])

        sq_rr = [0]

        def tile_stats(tile):
            for b in tile:
                i = sq_rr[0]
                sq_rr[0] ^= 1
                stats_block(b, wada[i], Rwada[i])

        def stats_sq(b):
            i = sq_rr[0]
            sq_rr[0] ^= 1
            c0, nb = bcols(b)
            act(wada[i][:, :, 0:nb], x_sb[:, :, c0:c0 + nb], AF.Square,
                [Rx[k][b] for k in range(KC)], [Rwada[i]])

            def pe_part():
                for k in range(KC):
                    mm(banks[SBK][0:nb, b:b + 1], wada[i][:, k, 0:nb], onesb[:, 0:1], k == 0, k == KC - 1,
                       [Rwada[i], Ronesb], [Rbank[SBK]], k == KC - 1)
            return pe_part

        def stats(blocks):
            tile_stats(blocks)
            stats_finish()

        diag_rr = [0]

        def rstd_diag(tile):
            parts = []
            for b in tile:
                c0, nb = bcols(b)
                if b == SBLK:
                    i = 2
                    dhi = diagb_s[0:nb, 0, 0:nb]
                    dlo = diagb_s[0:nb, 1, 0:nb]
                else:
                    i = diag_rr[0]
                    diag_rr[0] ^= 1
                    dhi = diagb[i][0:nb, 0, 0:nb]
                    dlo = diagb[i][0:nb, 1, 0:nb]
                dve(lambda e, dhi=dhi, b=b, nb=nb: e.tensor_scalar(
                    out=dhi, in0=identf[0:nb, 0:nb], scalar1=rstd_tok[0:nb, b:b + 1],
                    scalar2=None, op0=ALU.mult), [Rrstd, Rconst], [Rdiag[i]])
                dve(lambda e, dhi=dhi, dlo=dlo, b=b, nb=nb: e.scalar_tensor_tensor(
                    out=dlo, in0=identf[0:nb, 0:nb], scalar=rstd_tok[0:nb, b:b + 1], in1=dhi,
                    op0=ALU.mult, op1=ALU.subtract), [Rrstd, Rconst, Rdiag[i]], [Rdiag[i]])
                parts.append((i, nb, dhi, dlo))
            return parts

        def rstd_mm(parts):
            bk = 7
            o = 0
            for (i, nb, dhi, dlo) in parts:
                mm(banks[bk][:, o:o + nb], onesb[0:nb, :], dhi, True, False,
                   [Rdiag[i], Ronesb], [Rbank[bk]], False)
                mm(banks[bk][:, o:o + nb], onesb[0:nb, :], dlo, False, True,
                   [Rdiag[i], Ronesb], [Rbank[bk]], True)
                o += nb
            return bk

        def rstd_bcast(tile):
            bk = 7
            o = 0
            for j in range(0, len(tile), 2):
                parts = rstd_diag(tile[j:j + 2])
                for (i, nb, dhi, dlo) in parts:
                    mm(banks[bk][:, o:o + nb], onesb[0:nb, :], dhi, True, False,
                       [Rdiag[i], Ronesb], [Rbank[bk]], False)
                    mm(banks[bk][:, o:o + nb], onesb[0:nb, :], dlo, False, True,
                       [Rdiag[i], Ronesb], [Rbank[bk]], True)
                    o += nb
            return bk

        def segs(tile):
            out = []
            npr = sum(BT for b in tile if b != SBLK)
            if npr:
                out.append((0, npr, 0))
            if SBLK in tile:
                out.append((npr, 16, 1))
                out.append((npr + 16, 16, 2))
            return out

        tmp_rr = [0]

        def make_h_slices(l, which, tile, hdst, Rh):
            t0 = bcols(tile[0])[0]
            bk = rstd_bcast(tile)
            shbase = 0 if which == 0 else 24

            def mk(k):
                def f():
                    for (o, n, s) in segs(tile):
                        for c0 in range(o, o + n, 256):
                            c1 = min(c0 + 256, o + n)
                            i = tmp_rr[0]
                            tmp_rr[0] ^= 1
                            dve(lambda e, i=i, c0=c0, c1=c1: e.tensor_tensor(
                                out=tmp_m[i][:, 0:c1 - c0], in0=x_sb[:, k, t0 + c0:t0 + c1], in1=banks[bk][:, c0:c1],
                                op=ALU.mult),
                                [Rx[k][b] for b in tile] + [Rbank[bk]], [Rsg[i]])
                            act(hdst[:, k, c0:c1], tmp_m[i][:, 0:c1 - c0], AF.Identity,
                                [Rsg[i], Rgsc[l], RmodT[l]], [Rh[k]],
                                bias=modT[:, l, shbase + k, s:s + 1], scale=gsc[:, l, which, k, s:s + 1])
                return f
            return [mk(k) for k in range(KC)]

        def make_h(l, which, tile, hdst, Rh, bk_pre=None):
            t0 = bcols(tile[0])[0]
            N = sum(bcols(b)[1] for b in tile)
            bk = rstd_bcast(tile) if bk_pre is None else bk_pre
            shbase = 0 if which == 0 else 24
            for k in range(KC):
                for (o, n, s) in segs(tile):
                    for c0 in range(o, o + n, 256):
                        c1 = min(c0 + 256, o + n)
                        i = tmp_rr[0]
                        tmp_rr[0] ^= 1
                        dve(lambda e, i=i, k=k, bk=bk, c0=c0, c1=c1: e.tensor_tensor(
                            out=tmp_m[i][:, 0:c1 - c0], in0=x_sb[:, k, t0 + c0:t0 + c1], in1=banks[bk][:, c0:c1],
                            op=ALU.mult),
                            [Rx[k][b] for b in tile] + [Rbank[bk]], [Rsg[i]])
                        act(hdst[:, k, c0:c1], tmp_m[i][:, 0:c1 - c0], AF.Identity,
                            [Rsg[i], Rgsc[l], RmodT[l]], [Rh[k]],
                            bias=modT[:, l, shbase + k, s:s + 1], scale=gsc[:, l, which, k, s:s + 1])

        def load_x(between):
            sq_pend = []
            for b in list(range(NBLK)) + [SBLK]:
                c0, nb = bcols(b)
                i = b % 2
                dma("sp", stage[i][0:nb, :], xin[c0:c0 + nb, :], [], [Ra[i]], Da[i])
                for q in range(2):
                    bk = alloc_bank()
                    for j in range(4):
                        k = q * 4 + j
                        S.op("pe", lambda e, i=i, k=k, j=j, bk=bk, nb=nb: e.transpose(
                            out=banks[bk][:, j * 128:j * 128 + nb], in_=stage[i][0:nb, k * 128:(k + 1) * 128],
                            identity=identf[0:nb, 0:nb]),
                            reads=[Ra[i], Rconst], writes=[Rbank[bk]], signal=(j == 3))
                    copy_any(x_sb[:, q * 4:q * 4 + 4, c0:c0 + nb],
                             banks[bk][:, :].rearrange("p (j t) -> p j t", j=4)[:, :, 0:nb],
                             [Rbank[bk]], [Rx[k][b] for k in range(q * 4, q * 4 + 4)])
                for p in sq_pend:
                    p()
                del sq_pend[:]
                sq_pend.append(stats_sq(b))
                between(b)
            for p in sq_pend:
                p()

        ptok_rr = [0]
        vtok_rr = [0]

        def new_state(l, ti, tile):
            N = sum(bcols(b)[1] for b in tile)
            return {"tile": tile, "N": N, "hb": ti % 2, "ub": ti % 2, "ptok": {}, "vtok": {}, "l": l,
                    "t0": bcols(tile[0])[0]}

        def stage_diag(st):
            st["diag"] = rstd_diag(st["tile"])

        def stage_rbc(st):
            if "diag" not in st:
                stage_diag(st)
            st["rbc"] = rstd_mm(st["diag"])

        def stage_h(st):
            if "rbc" not in st:
                stage_rbc(st)
            make_h(st["l"], 0, st["tile"], h_m[st["hb"]], Rh_m[st["hb"]], bk_pre=st["rbc"])

        def stage_u(st, slots):
            wu, N, hb, ub = slots["u"], st["N"], st["hb"], st["ub"]
            for n in range(4):
                bk = alloc_bank()
                for k in range(KC):
                    mm(banks[bk][:, 0:N], wu[1][:, k, n * 128:(n + 1) * 128], h_m[hb][:, k, 0:N],
                       k == 0, k == KC - 1, [Rring[wu[0]], Rh_m[hb][k]], [Rbank[bk]], k == KC - 1)
                act(u_m[ub][:, n, 0:N], banks[bk][:, 0:N], AF.Gelu_apprx_tanh, [Rbank[bk]], [Ru_m[ub]])

        def stage_pv(st, slots, p_only=False):
            l, tile, hb = st["l"], st["tile"], st["hb"]
            wp, wv = slots["p"], slots["v"]
            o = 0
            gl = []
            for bi, b in enumerate(tile):
                c0, nb = bcols(b)
                bkp = alloc_bank()
                for k in range(KC):
                    mm(banks[bkp][0:nb, :], h_m[hb][:, k, o:o + nb], wp[1][:, k, :], k == 0, k == KC - 1,
                       [Rring[wp[0]], Rh_m[hb][k]], [Rbank[bkp]], k == KC - 1)
                if b == SBLK:
                    pt, Rpt = ptok_s, Rptok_s
                else:
                    pi = ptok_rr[0]
                    ptok_rr[0] = (pi + 1) % NPT
                    pt, Rpt = ptok[pi], Rptok[pi]
                st["ptok"][b] = (pt, Rpt)
                if b in (NBLK - 1, SBLK) and not p_only:
                    dve(lambda e, pt=pt, nb=nb, bkp=bkp: e.tensor_copy(out=pt[0:nb, :], in_=banks[bkp][0:nb, :]),
                        [Rbank[bkp]], [Rpt])
                    dve(lambda e, nb=nb, bkp=bkp: e.tensor_copy(out=pf32[0:nb, :], in_=banks[bkp][0:nb, :]),
                        [Rbank[bkp]], [Ra[1]])
                    if b == SBLK:
                        for j in range(2):
                            out_toks.append(dma("sp", sp_s[l, j], pf32[16 * j + 1:16 * j + 16, :], [Ra[1]],
                                                [Rdram_out], Da[1]))
                    else:
                        out_toks.append(dma("sp", sp_p[l], pf32[113:128, :], [Ra[1]], [Rdram_out], Da[1]))
                else:
                    copy_any(pt[0:nb, :], banks[bkp][0:nb, :], [Rbank[bkp]], [Rpt])
                if p_only:
                    o += nb
                    continue
                bkv = alloc_bank()
                for k in range(KC):
                    mm(banks[bkv][0:nb, :], h_m[hb][:, k, o:o + nb], wv[1][:, k, :], k == 0, k == KC - 1,
                       [Rring[wv[0]], Rh_m[hb][k]], [Rbank[bkv]], k == KC - 1)
                gi = bi % 2
                if len(gl) == 2:
                    finish_ln(st, gl)
                    gl = []
                act(g_m[gi][0:nb, :], banks[bkv][0:nb, :], AF.Gelu_apprx_tanh, [Rbank[bkv]], [Rg_m[gi]])
                dve(lambda e, gi=gi, bi=bi, nb=nb: e.bn_stats(out=bnst[0:nb, bi, :], in_=g_m[gi][0:nb, :]),
                    [Rg_m[gi]], [Rbn])
                dve(lambda e, bi=bi, nb=nb: e.bn_aggr(out=mv[0:nb, bi, :], in_=bnst[0:nb, bi, :]),
                    [Rbn], [Rmv])
                gl.append((b, bi, gi, nb))
                o += nb
            if gl:
                finish_ln(st, gl)

        def finish_ln(st, gl):
            l = st["l"]
            b_lo, b_hi = gl[0][1], gl[-1][1] + 1
            act(sv[:, b_lo:b_hi], mv[:, b_lo:b_hi, 1], AF.Sqrt, [Rmv, Reps], [Rrv],
                bias=eps_t[:, 0:1], scale=1.0)
            dve(lambda e: e.reciprocal(out=rv[:, b_lo:b_hi], in_=sv[:, b_lo:b_hi]), [Rrv], [Rrv])
            for (b, bi, gi, nb) in gl:
                dve(lambda e, gi=gi, bi=bi, nb=nb: e.scalar_tensor_tensor(
                    out=g_m[gi][0:nb, :], in0=g_m[gi][0:nb, :], scalar=mv[0:nb, bi, 0:1], in1=lng_bc[0:nb, :],
                    op0=ALU.subtract, op1=ALU.mult), [Rg_m[gi], Rmv, Rln], [Rg_m[gi]])
                if b == SBLK:
                    st["vtok"][b] = (vtok_s, Rvtok_s)
                    dve(lambda e, gi=gi, bi=bi, nb=nb: e.scalar_tensor_tensor(
                        out=vf32[0:nb, :], in0=g_m[gi][0:nb, :], scalar=rv[0:nb, bi:bi + 1], in1=lnb_bc[0:nb, :],
                        op0=ALU.mult, op1=ALU.add), [Rg_m[gi], Rrv, Rln], [Ra[1]])
                    dve(lambda e, nb=nb: e.tensor_copy(out=vtok_s[0:nb, :], in_=vf32[0:nb, :]),
                        [Ra[1]], [Rvtok_s])
                    for j in range(2):
                        out_toks.append(dma("sp", sv_s[l, j], vf32[16 * j:16 * j + 16, :], [Ra[1]],
                                            [Rdram_out], Da[1]))
                else:
                    vi = vtok_rr[0]
                    vtok_rr[0] = (vi + 1) % 4
                    st["vtok"][b] = (vtok[vi], Rvtok[vi])
                    dve(lambda e, gi=gi, bi=bi, vi=vi, nb=nb: e.scalar_tensor_tensor(
                        out=vtok[vi][0:nb, :], in0=g_m[gi][0:nb, :], scalar=rv[0:nb, bi:bi + 1],
                        in1=lnb_bc[0:nb, :], op0=ALU.mult, op1=ALU.add),
                        [Rg_m[gi], Rrv, Rln], [Rvtok[vi]])

        def stage_pool(st, prev_ptok):
            tile, ub = st["tile"], st["ub"]
            o = 0
            for b in tile:
                c0, nb = bcols(b)
                if b == SBLK:
                    ppv, Rpp, kp, npv, bmain, bprev = pprev_s, Rpprev, 32, 32, bms, bps
                else:
                    ppv, Rpp = prev_ptok[b]
                    kp, npv = 128, 16
                    bmain, bprev = (bands["b4m"], bandsp["b4p"]) if b == 4 else (bands["bm"], bandsp["bp"])
                pc, Rpc = st["ptok"][b]
                bk = alloc_bank()
                for g in range(4):
                    mm(banks[bk][:, g * 128:g * 128 + nb], pc[0:nb, g * 128:(g + 1) * 128], bmain[0:nb, g, 0:nb],
                       True, False, [Rpc, Rconstp], [Rbank[bk]], False)
                    mm(banks[bk][:, g * 128:g * 128 + npv], ppv[0:kp, g * 128:(g + 1) * 128], bprev[0:kp, g, 0:npv],
                       False, True, [Rpp, Rconstp], [Rbank[bk]], g == 3)
                copy_any(d_m[:, :, o:o + nb],
                         banks[bk][:, :].rearrange("p (g t) -> p g t", g=4)[:, :, 0:nb],
                         [Rbank[bk]], [Rd_m])
                o += nb
        def stage_gmlp(st):
            tile, ub = st["tile"], st["ub"]
            o = 0
            for b in tile:
                c0, nb = bcols(b)
                vt, Rvt = st["vtok"][b]
                bk = alloc_bank()
                for hd in range(4):
                    if b == SBLK:
                        wm_ap, Rw = wmT_s[0:nb, hd, 0:nb], Rwms
                        bs_ap = rows[0:1, 512 + hd * 32:512 + hd * 32 + nb]
                    else:
                        wm_ap, Rw = wmT[:, hd, :], Rwm
                        bs_ap = rows[0:1, hd * 128:(hd + 1) * 128]
                    mm(banks[bk][:, hd * 128:hd * 128 + nb], vt[0:nb, hd * 128:(hd + 1) * 128], wm_ap,
                       True, False, [Rvt, Rw], [Rbank[bk]], False)
                    mm(banks[bk][:, hd * 128:hd * 128 + nb], onesb[0:1, :], bs_ap, False, True,
                       [Rsmall, Ronesb], [Rbank[bk]], hd == 3)
                dve(lambda e, bk=bk, o=o, nb=nb, ub=ub: e.tensor_tensor(
                    out=ycat[:, 4:8, o:o + nb], in0=u_m[ub][:, :, o:o + nb],
                    in1=banks[bk][:, :].rearrange("p (h t) -> p h t", h=4)[:, :, 0:nb], op=ALU.mult),
                    [Rbank[bk], Ru_m[ub]], [Rycat])
                o += nb

        def stage_wpool(st):
            l, N = st["l"], st["N"]
            for g in range(4):
                bk = alloc_bank()
                mm(banks[bk][:, 0:N], wpool_sb[:, g, :], d_m[:, g, 0:N], True, True,
                   [Rsmall, Rd_m], [Rbank[bk]], True)
                act(ycat[:, g, 0:N], banks[bk][:, 0:N], AF.Identity, [Rbank[bk], Rconst2], [Rycat],
                    scale=psT[:, l, g:g + 1])

        def stage_out(st, slots):
            l, tile, N, t0 = st["l"], st["tile"], st["N"], st["t0"]
            wo = slots["o"]
            for n in range(KC):
                bk = alloc_bank()
                wsl = wo[n // 4]
                for k in range(KC):
                    mm(banks[bk][:, 0:N], wsl[1][:, k, (n % 4) * 128:(n % 4 + 1) * 128], ycat[:, k, 0:N],
                       k == 0, k == KC - 1, [Rring[wsl[0]], Rycat], [Rbank[bk]], k == KC - 1)
                for (so, sn, s) in segs(tile):
                    blks = [b for b in tile if b != SBLK] if s == 0 else [SBLK]
                    dve(lambda e, bk=bk, n=n, so=so, sn=sn, s=s: e.scalar_tensor_tensor(
                        out=x_sb[:, n, t0 + so:t0 + so + sn], in0=banks[bk][:, so:so + sn],
                        scalar=modT[:, l, 16 + n, s:s + 1], in1=x_sb[:, n, t0 + so:t0 + so + sn],
                        op0=ALU.mult, op1=ALU.add),
                        [Rbank[bk], RmodT[l]] + [Rx[n][b] for b in blks], [Rx[n][b] for b in blks])

        def mixer_phase(l, mixer_bg=lambda: None, drain_hook=lambda: None):
            full = list(range(l + 1, NBLK)) + [SBLK]
            slots = {}
            slots["p"] = load_cols(w_in, l, 0, 512)
            slots["u"] = load_cols(w_in, l, 512, 512)
            slots["v"] = load_cols(w_in, l, 1024, 512)
            slots["o"] = [load_cols(w_out, l, 0, 512), load_cols(w_out, l, 512, 512)]
            S.barrier()
            stats_finish()
            tiles = tiles_of(full, 2)
            if tiles[-1] == [SBLK]:
                tiles = tiles[:-2] + [tiles[-2] + [SBLK]]
            st0 = new_state(l, -1, [l])
            stage_h(st0)
            stage_pv(st0, slots, p_only=True)
            prev_ptok = {l + 1: st0["ptok"][l]}
            n = len(tiles)
            sts = [new_state(l, i, tile) for i, tile in enumerate(tiles)]
            stage_h(sts[0])
            prep_wm(l)
            for i in range(n + 2):
                if i == n:
                    drain_hook()
                if i + 1 < n:
                    stage_diag(sts[i + 1])
                if i < n:
                    stage_u(sts[i], slots)
                if 0 <= i - 2 < n:
                    stage_out(sts[i - 2], slots)
                if 0 <= i - 3 < n:
                    tile_stats(sts[i - 3]["tile"])
                if i + 1 < n:
                    stage_rbc(sts[i + 1])
                if 0 <= i - 1 < n:
                    stage_pool(sts[i - 1], prev_ptok)
                    stage_gmlp(sts[i - 1])
                if i + 1 < n:
                    stage_h(sts[i + 1])
                if 0 <= i - 1 < n:
                    stage_wpool(sts[i - 1])
                if i < n:
                    stage_pv(sts[i], slots)
                    for b in sts[i]["tile"]:
                        if b != SBLK and b + 1 < NBLK:
                            prev_ptok[b + 1] = sts[i]["ptok"][b]
                mixer_bg()
            tile_stats(sts[n - 1]["tile"])

        def ffn_begin(l):
            full = list(range(l + 1, NBLK)) + [SBLK]
            tiles = tiles_of(full, 4)
            ctx = {"l": l, "tiles": tiles}

            def load_ffn_piece(j):
                nch = min(4, NFC - 4 * j)
                g = load_cols(w_gate, l, j * 512, nch * 128)
                u = load_cols(w_up, l, j * 512, nch * 128)
                dn = load_rows(w_down, l, j * 512, nch)
                return (nch, g, u, dn)

            ctx["load"] = load_ffn_piece
            ctx["pieces"] = [load_ffn_piece(0)]
            Rh = [[Res(f"hall{l}_{ti}_{k}") for k in range(KC)] for ti in range(len(tiles))]
            ctx["Rh"] = Rh

            def mkh(ti):
                tile = tiles[ti]
                N = sum(bcols(b)[1] for b in tile)
                make_h(l, 1, tile, h_tile(ti, N), Rh[ti])

            ctx["mkh"] = mkh

            def mkh_slices(ti):
                tile = tiles[ti]
                N = sum(bcols(b)[1] for b in tile)
                return make_h_slices(l, 1, tile, h_tile(ti, N), Rh[ti])

            ctx["mkh_slices"] = mkh_slices
            seed(Rh[0], Rh_m[0] + Rh_m[1])
            c1 = tiles[1][-1] if len(tiles) > 1 and tiles[1][-1] != SBLK else tiles[0][-1]
            stats_finish(tiles[0][0], c1 + 1)
            mkh(0)
            return ctx

        def ffn_phase(ctx, next_layer_work):
            l, tiles, Rh, mkh, pieces = ctx["l"], ctx["tiles"], ctx["Rh"], ctx["mkh"], ctx["pieces"]
            load_ffn_piece = ctx["load"]
            npieces = (NFC + 3) // 4
            regions = [None,
                       Rh_m[0] + Rh_m[1] + [Ru_m[0], Ru_m[1], Rycat],
                       [Rycat, Rd_m, Rptok_s] + Rptok,
                       [Rptok_s, Rvtok_s, Rg_m[0]] + Rptok + Rvtok,
                       [Rvtok_s, Rg_m[0], Rg_m[1]] + Rvtok]
            for ti in range(1, len(tiles)):
                seed(Rh[ti], regions[ti])
            mkh_slices = ctx["mkh_slices"]

            for j in range(npieces):
                if j + 1 < npieces:
                    pieces.append(load_ffn_piece(j + 1))
                nch, wg, wu, wd = pieces[j]

                def GU(ti, hook=None):
                    tile = tiles[ti]
                    N = sum(bcols(b)[1] for b in tile)
                    hT = h_tile(ti, N)
                    ab = ti % 2
                    for c in range(nch):
                        for _ in range(2):
                            if hook:
                                hook.pop(0)()
                        bg = alloc_bank()
                        for k in range(KC):
                            mm(banks[bg][:, 0:N], wg[1][:, k, c * 128:(c + 1) * 128], hT[:, k, 0:N],
                               k == 0, k == KC - 1, [Rring[wg[0]], Rh[ti][k]], [Rbank[bg]], k == KC - 1)
                        bu = alloc_bank()
                        for k in range(KC):
                            mm(banks[bu][:, 0:N], wu[1][:, k, c * 128:(c + 1) * 128], hT[:, k, 0:N],
                               k == 0, k == KC - 1, [Rring[wu[0]], Rh[ti][k]], [Rbank[bu]], k == KC - 1)
                        si = c % 2
                        act(sg_sb[si][:, 0:N], banks[bg][:, 0:N], AF.Silu, [Rbank[bg]], [Rsg[si]])
                        dve(lambda e, ab=ab, c=c, si=si, bu=bu, N=N: e.tensor_tensor(
                            out=a_sb[ab][:, c, 0:N], in0=sg_sb[si][:, 0:N], in1=banks[bu][:, 0:N], op=ALU.mult),
                            [Rsg[si], Rbank[bu]], [Ra[ab]])

                def DN(ti):
                    tile = tiles[ti]
                    t0 = bcols(tile[0])[0]
                    N = sum(bcols(b)[1] for b in tile)
                    ab = ti % 2
                    for n in range(KC):
                        bk = alloc_bank()
                        for c in range(nch):
                            mm(banks[bk][:, 0:N], wd[1][:, c, n * 128:(n + 1) * 128], a_sb[ab][:, c, 0:N],
                               c == 0, c == nch - 1, [Rring[wd[0]], Ra[ab]], [Rbank[bk]], c == nch - 1)
                        for (so, sn, s) in segs(tile):
                            blks = [b for b in tile if b != SBLK] if s == 0 else [SBLK]
                            dve(lambda e, bk=bk, n=n, so=so, sn=sn, s=s, t0=t0: e.scalar_tensor_tensor(
                                out=x_sb[:, n, t0 + so:t0 + so + sn], in0=banks[bk][:, so:so + sn],
                                scalar=modT[:, l, 40 + n, s:s + 1], in1=x_sb[:, n, t0 + so:t0 + so + sn],
                                op0=ALU.mult, op1=ALU.add),
                                [Rbank[bk], RmodT[l]] + [Rx[n][b] for b in blks], [Rx[n][b] for b in blks])

                if j == 0:
                    GU(0, mkh_slices(1) if len(tiles) > 1 else None)
                    stats_finish()
                else:
                    GU(0)
                last = (j == npieces - 1)
                for ti in range(len(tiles)):
                    sblk = list(tiles[ti - 1]) if (last and ti >= 1) else []
                    pend = [stats_sq(b) for b in sblk[0:2]]
                    if ti + 1 < len(tiles):
                        if j == 0 and ti + 2 < len(tiles):
                            GU(ti + 1, mkh_slices(ti + 2))
                        else:
                            GU(ti + 1)
                    for p in pend:
                        p()
                    pend = [stats_sq(b) for b in sblk[2:4]]
                    DN(ti)
                    for p in pend:
                        p()
                    for b in sblk[4:]:
                        stats_sq(b)()
                    next_layer_work(2)
                if last:
                    tile_stats(tiles[-1])

        def final_phase():
            blocks = list(range(4, NBLK)) + [SBLK]
            S.barrier()
            stats_finish()
            gi = alloc_ring()
            gfin_bc = ring[gi][:, 0:2048].bitcast(F32)
            dma("sp", gfin_bc[:, :], g_final.partition_broadcast(128), [], [Rring[gi]], Dgfin)
            for bi, b in enumerate(blocks):
                c0, nb = bcols(b)
                i = bi % 2
                for q in range(2):
                    bk = alloc_bank()
                    for j in range(4):
                        k = q * 4 + j
                        S.op("pe", lambda e, k=k, j=j, bk=bk, nb=nb, c0=c0: e.transpose(
                            out=banks[bk][0:nb, j * 128:(j + 1) * 128], in_=x_sb[:, k, c0:c0 + nb],
                            identity=identf[:, :]),
                            reads=[Rx[k][b], Rconst], writes=[Rbank[bk]], signal=(j == 3))
                    dve(lambda e, i=i, q=q, bk=bk, nb=nb, b=b: e.scalar_tensor_tensor(
                        out=stage[i][0:nb, q * 512:(q + 1) * 512], in0=banks[bk][0:nb, :],
                        scalar=rstd_tok[0:nb, b:b + 1], in1=gfin_bc[0:nb, q * 512:(q + 1) * 512],
                        op0=ALU.mult, op1=ALU.mult), [Rbank[bk], Rrstd, Rring[gi]], [Ra[i]])
                r0 = (b - 4) * BT if b != SBLK else 16 * BT
                out_toks.append(dma("sp", y_out[r0:r0 + nb, :], stage[i][0:nb, :], [Ra[i]], [Rdram_out], Da[i]))

        for g in range(6):
            ri, rview = load_cols(w_ada, 0, g * 512, 512)
            mod_queue.extend((0, 4 * g + jj, (ri, rview)) for jj in range(4))
        mod_queue.extend((0, j, None) for j in range(24, 48))
        def ring_mods(b):
            if b >= 2 and gain_loads:
                gain_loads.pop(0)()
            if b >= 8:
                for _ in range(2):
                    if mod_queue and mod_queue[0][2] is not None:
                        mod_work(1)
        load_x(ring_mods)
        while gain_loads:
            gain_loads.pop(0)()
        while mod_queue and mod_queue[0][1] < 24:
            mod_work(1)
        load_small(0)
        for l in range(n_layers):
            holder = []
            mixer_phase(l, lambda: mod_work(3), lambda l=l, holder=holder: holder.append(ffn_begin(l)))
            mod_work(48)
            if l + 1 < n_layers:
                mod_queue.extend((l + 1, j, None) for j in range(48))
            first = [True]

            def nlw(n, l=l, first=first):
                if l + 1 < n_layers:
                    if first[0]:
                        load_small(l + 1)
                        first[0] = False
                    mod_work(n)
            ffn_phase(holder[0], nlw)
            mod_work(48)
        if do_final:
            final_phase()
        fin = {}
        for t in out_toks:
            if id(t.sem) not in fin or fin[id(t.sem)][1] < t.val:
                fin[id(t.sem)] = (t.sem, t.val)
        S.eng["sp"].ops.append((list(fin.values()), None, None))

        with nc.Block() as block:
            @block.tensor
            def _(e):
                S.replay("pe", e)

            @block.scalar
            def _(e):
                S.replay("act", e)

            @block.vector
            def _(e):
                S.replay("dve", e)

            @block.gpsimd
            def _(e):
                S.replay("pool", e)

            @block.sync
            def _(e):
                S.replay("sp", e)
    return nc


def _band_consts():
    s = np.arange(128)[:, None]
    t = np.arange(128)[None, :]
    bm = np.zeros((128, 4, 128), np.float32)
    bp = np.zeros((128, 4, 128), np.float32)
    b4m = np.zeros((128, 4, 128), np.float32)
    for g, w in enumerate(POOL_W):
        inwin = ((t - s) >= 0) & ((t - s) < w)
        bm[:, g, :] = inwin / float(w) - (s == t)
        bp[:, g, :] = (((t + 128 - s) >= 0) & ((t + 128 - s) < w)) / float(w)
        cnt = np.minimum(t + 1, w).astype(np.float32)
        b4m[:, g, :] = inwin / cnt - (s == t)
    bp = np.ascontiguousarray(bp[:, :, 0:16])
    b4p = np.zeros_like(bp)
    s = np.arange(32)[:, None]
    t = np.arange(32)[None, :]
    bms = np.zeros((32, 4, 32), np.float32)
    bps = np.zeros((32, 4, 32), np.float32)
    same = (s // 16) == (t // 16)
    r = s % 16
    tl = t % 16
    for g, w in enumerate(POOL_W):
        bms[:, g, :] = (same & ((t - s) >= 0) & ((t - s) < w)) / float(w) - (s == t)
        bps[:, g, :] = (same & (r < 15) & ((tl + 15 - r) < w)) / float(w)
    return bm, bp, b4m, b4p, bms, bps


_NC_CACHE = {}


def kernel(x_prompt, x_sample, c_prompt, c_sample, cache_pool, w_ada, b_ada, g_mix, w_in,
           w_pool, pool_scale, ln_g, ln_b, w_s, b_s, w_out, g_ffn, w_gate, w_up, w_down, g_final):
    f = lambda a: np.ascontiguousarray(np.asarray(a), dtype=np.float32)
    x_prompt, x_sample, c_prompt, c_sample, cache_pool = map(f, (x_prompt, x_sample, c_prompt, c_sample, cache_pool))
    shared = dict(w_ada=f(w_ada), b_ada=f(b_ada), g_mix=f(g_mix), w_in=f(w_in), w_pool=f(w_pool),
                  pool_scale=f(pool_scale), ln_g=f(ln_g), ln_b=f(ln_b), w_s=f(w_s), b_s=f(b_s),
                  w_out=f(w_out), g_ffn=f(g_ffn), w_gate=f(w_gate), w_up=f(w_up), w_down=f(w_down),
                  g_final=f(g_final))
    bm, bp, b4m, b4p, bms, bps = _band_consts()
    shared.update(bm=bm, bp=bp, bms=bms, bps=bps, ident=np.eye(128, dtype=np.float32))
    in_maps = []
    for c in range(8):
        b, half = c // 2, c % 2
        xin = np.zeros((TT, D), np.float32)
        if half == 0:
            xin[4 * BT:NBLK * BT] = x_prompt[b, 0:16 * BT]
        else:
            xin[0:NBLK * BT] = x_prompt[b, 12 * BT:32 * BT]
        xin[NBLK * BT:NBLK * BT + 16] = x_sample[2 * c]
        xin[NBLK * BT + 16:] = x_sample[2 * c + 1]
        m = dict(shared)
        m["xin"] = xin
        m["c3"] = np.stack([c_prompt[b], c_sample[2 * c], c_sample[2 * c + 1]])
        m["cache"] = np.ascontiguousarray(cache_pool[:, 2 * c:2 * c + 2])
        m["b4m"] = b4m if half == 0 else bm
        m["b4p"] = b4p if half == 0 else bp
        in_maps.append(m)
    if "nc" not in _NC_CACHE:
        _NC_CACHE["nc"] = build()
    res = run_bass_kernel_spmd(_NC_CACHE["nc"], in_maps, core_ids=list(range(8)))
    R = res.results
    B, SEQ = x_prompt.shape[0], x_prompt.shape[1]
    y_prompt = np.zeros((B, SEQ, D), np.float32)
    y_sample = np.zeros((16, 16, D), np.float32)
    spp = np.zeros((DEPTH, B, 15, 512), np.float32)
    sps = np.zeros((DEPTH, 16, 15, 512), np.float32)
    svs = np.zeros((DEPTH, 16, 16, 512), np.float32)
    for c in range(8):
        b, half = c // 2, c % 2
        yo = R[c]["y_out"]
        y_prompt[b, half * 2048:(half + 1) * 2048] = yo[0:2048]
        y_sample[2 * c] = yo[2048:2064]
        y_sample[2 * c + 1] = yo[2064:2080]
        if half == 1:
            spp[:, b] = R[c]["sp_p"]
        sps[:, 2 * c:2 * c + 2] = R[c]["sp_s"]
        svs[:, 2 * c:2 * c + 2] = R[c]["sv_s"]
    return (y_prompt, y_sample, spp, sps, svs)
```

```python
import numpy as np
from contextlib import ExitStack
import concourse.bass as bass
import concourse.mybir as mybir
from concourse.bass_utils import run_bass_kernel_spmd

F32, BF16 = mybir.dt.float32, mybir.dt.bfloat16
AF = mybir.ActivationFunctionType
ALU = mybir.AluOpType

D = 1024
KC = 8
DEPTH = 4
DFF = 2816
NFC = 22
NBLK = 20
BT = 128
NSMP = 32
SBLK = NBLK
TT = NBLK * BT + NSMP
HT = (NBLK - 1) * BT + NSMP
EPS = 1e-6
POOL_W = (2, 4, 8, 16)
NRING = 6


class Tok:
    __slots__ = ("sem", "val")

    def __init__(self, sem, val):
        self.sem, self.val = sem, val


class Res:
    __slots__ = ("name", "w", "r")

    def __init__(self, name):
        self.name, self.w, self.r = name, None, []


class DmaSem:
    def __init__(self, sem):
        self.sem, self.count = sem, 0


class EngS:
    def __init__(self, name, sem):
        self.name, self.sem = name, sem
        self.count = 0
        self.ops = []
        self.known = {}
        self.pending = []


class Sched:
    def __init__(self):
        self.eng = {}

    def add_engine(self, name, sem):
        self.eng[name] = EngS(name, sem)

    def op(self, eng, emit, reads=(), writes=(), signal=True, dma=None):
        E = self.eng[eng]
        dsem_obj = dma.sem if dma is not None else None
        deps = []
        for r in reads:
            if r.w is not None:
                deps.append(r.w)
        for w in writes:
            if w.w is not None:
                deps.append(w.w)
            deps.extend(w.r)
        waits = {}
        for t in deps:
            if eng == "pe" and t.sem is E.sem:
                continue
            assert t.val is not None, "dependency on an unsignaled op"
            if dsem_obj is not None and t.sem is dsem_obj:
                continue
            if E.known.get(id(t.sem), 0) >= t.val:
                continue
            k = id(t.sem)
            if k not in waits or waits[k][1] < t.val:
                waits[k] = (t.sem, t.val)
        for k, (s, v) in waits.items():
            E.known[k] = v
        if dma is not None:
            dma.count += 16
            tok = Tok(dma.sem, dma.count)
            inc = (dma.sem, 16)
        elif signal:
            E.count += 1
            tok = Tok(E.sem, E.count)
            inc = (E.sem, 1)
            for p in E.pending:
                p.val = E.count
            E.pending = []
        else:
            tok = Tok(E.sem, None)
            E.pending.append(tok)
            inc = None
        E.ops.append((list(waits.values()), emit, inc))
        for r in reads:
            r.r.append(tok)
        for w in writes:
            w.w = tok
            w.r = []
        return tok

    def barrier(self):
        snap = {n: (E.sem, E.count) for n, E in self.eng.items()}
        for n, E in self.eng.items():
            assert not E.pending
            waits = []
            for m, (s, v) in snap.items():
                if m != n and v > 0 and E.known.get(id(s), 0) < v:
                    waits.append((s, v))
                    E.known[id(s)] = v
            E.ops.append((waits, None, None))

    def replay(self, name, handle):
        E = self.eng[name]
        for waits, emit, inc in E.ops:
            for (s, v) in waits:
                handle.wait_ge(s, v)
            if emit is None:
                continue
            inst = emit(handle)
            if inc is not None:
                inst.then_inc(inc[0], inc[1])


def tiles_of(blocks, per):
    full = [b for b in blocks if b != SBLK]
    groups = [full[i:i + per] for i in range(0, len(full), per)]
    if SBLK in blocks:
        if groups and len(groups[-1]) < per:
            groups[-1] = groups[-1] + [SBLK]
        else:
            groups.append([SBLK])
    return groups


def bcols(b):
    if b == SBLK:
        return NBLK * BT, NSMP
    return b * BT, BT


def build(n_layers=DEPTH, do_final=True, wdepth=DEPTH):
    nc = bass.Bass("TRN2", target_bir_lowering=False)

    def din(name, shape):
        return nc.dram_tensor(name, list(shape), F32, kind="ExternalInput").ap()

    def dout(name, shape):
        return nc.dram_tensor(name, list(shape), F32, kind="ExternalOutput").ap()

    xin = din("xin", [TT, D])
    c3 = din("c3", [3, D])
    cache = din("cache", [wdepth, 2, 15, 512])
    band_in = {n: din(n, [128, 4, 128]) for n in ("bm", "b4m")}
    bandp_in = {n: din(n, [128, 4, 16]) for n in ("bp", "b4p")}
    bms_in = din("bms", [32, 4, 32])
    bps_in = din("bps", [32, 4, 32])
    ident_in = din("ident", [128, 128])
    w_ada = din("w_ada", [wdepth, D, 6 * D])
    b_ada = din("b_ada", [wdepth, 6 * D])
    g_mix = din("g_mix", [wdepth, D])
    w_in = din("w_in", [wdepth, D, 1536])
    w_pool = din("w_pool", [wdepth, 4, 128, 128])
    pool_scale = din("pool_scale", [wdepth, 512])
    ln_g = din("ln_g", [wdepth, 512])
    ln_b = din("ln_b", [wdepth, 512])
    w_s = din("w_s", [wdepth, 4, 128, 128])
    b_s = din("b_s", [wdepth, 4, 128])
    w_out = din("w_out", [wdepth, D, D])
    g_ffn = din("g_ffn", [wdepth, D])
    w_gate = din("w_gate", [wdepth, D, DFF])
    w_up = din("w_up", [wdepth, D, DFF])
    w_down = din("w_down", [wdepth, DFF, D])
    g_final = din("g_final", [D])

    y_out = dout("y_out", [16 * BT + NSMP, D])
    sp_p = dout("sp_p", [DEPTH, 15, 512])
    sp_s = dout("sp_s", [DEPTH, 2, 15, 512])
    sv_s = dout("sv_s", [DEPTH, 2, 16, 512])

    es = ExitStack()
    with es:
        def sb(name, shape, dt):
            return es.enter_context(nc.sbuf_tensor(name, list(shape), dt))

        x_sb = sb("x_sb", [128, KC, TT], F32)
        ring = [sb(f"ring{i}", [128, 4096], BF16) for i in range(NRING)]
        hreg = sb("hreg", [128, KC * HT], BF16)
        wada = [sb(f"wada{i}", [128, KC, 128], BF16) for i in range(2)]
        brT = [sb(f"brT{i}", [128, 1], F32) for i in range(2)]
        a_sb = [sb(f"a{i}", [128, 4, 512], BF16) for i in range(2)]
        sg_sb = [sb(f"sg{i}", [128, 512], BF16) for i in range(2)]
        identf = sb("identf", [128, 128], F32)
        onesf = sb("onesf", [128, 128], F32)
        onesb = sb("onesb", [128, 128], BF16)
        bands = {n: sb("band_" + n, [128, 4, 128], BF16) for n in ("bm", "b4m")}
        bandsp = {n: sb("band_" + n, [128, 4, 16], BF16) for n in ("bp", "b4p")}
        bms = sb("bms_sb", [32, 4, 32], BF16)
        bps = sb("bps_sb", [32, 4, 32], BF16)
        modT = sb("modT", [128, DEPTH, 48, 3], F32)
        gsc = sb("gsc", [128, DEPTH, 2, KC, 3], F32)
        gT = sb("gT", [128, 2, DEPTH, KC], F32)
        cT = sb("cT", [128, KC, 3], F32)
        scT = sb("scT", [128, KC, 3], BF16)
        psT = sb("psT", [128, DEPTH, 4], F32)
        wpool_sb = sb("wpool", [128, 4, 128], BF16)
        wmT = sb("wmT", [128, 4, 128], BF16)
        wmT_s = sb("wmT_s", [32, 4, 32], BF16)
        rows = sb("rows", [1, 4 * 128 + 4 * 32], BF16)
        lng_bc = sb("lng_bc", [128, 512], F32)
        lnb_bc = sb("lnb_bc", [128, 512], F32)
        pprev_s = sb("pprev_s", [32, 512], BF16)
        rstd_tok = sb("rstd_tok", [128, 24], F32)
        srt_tok = sb("srt_tok", [128, 24], F32)
        bnst = sb("bnst", [128, 4, 6], F32)
        mv = sb("mv", [128, 4, 2], F32)
        rv = sb("rv", [128, 4], F32)
        sv = sb("sv", [128, 4], F32)
        diagb = [sb(f"diagb{i}", [128, 2, 128], BF16) for i in range(2)]
        diagb_s = sb("diagb_s", [32, 2, 32], BF16)
        eps_t = sb("eps_t", [128, 1], F32)

        hoff = [0]

        def hview(nelem_bf16):
            o = hoff[0]
            hoff[0] += nelem_bf16
            assert hoff[0] <= KC * HT, "hreg overflow %d" % hoff[0]
            return hreg[:, o:o + nelem_bf16]

        NM = 288
        h_m = [hview(KC * NM).rearrange("p (k t) -> p k t", k=KC) for _ in range(2)]
        u_m = [hview(4 * NM).rearrange("p (k t) -> p k t", k=4) for _ in range(2)]
        ycat = hview(KC * NM).rearrange("p (k t) -> p k t", k=KC)
        d_m = hview(4 * NM).rearrange("p (k t) -> p k t", k=4)
        NPT = 5
        ptok = [hview(512) for _ in range(NPT)]
        ptok_s = hview(512)
        vtok = [hview(512) for _ in range(4)]
        vtok_s = hview(512)
        g_m = [hview(2 * 512).bitcast(F32) for _ in range(2)]
        def h_tile(ti, N):
            return hreg[:, ti * 4096:ti * 4096 + KC * N].rearrange("p (k t) -> p k t", k=KC)

        def seed(dst, srcs):
            toks = []
            for s in srcs:
                if s.w is not None:
                    toks.append(s.w)
                toks.extend(s.r)
            for d in dst:
                d.r.extend(toks)
        tmp_m = [sg_sb[i][:, :].bitcast(F32) for i in range(2)]
        tmp_f = [a_sb[i][:, :, :].rearrange("p a b -> p (a b)").bitcast(F32) for i in range(2)]
        stage = tmp_f
        pf32 = tmp_f[1][:, 0:512]
        vf32 = tmp_f[1][:, 512:1024]
        wsf = g_m[1].rearrange("p (h s) -> p h s", h=4)

        banks = [es.enter_context(nc.psum_tensor(f"bank{i}", [128, 512], F32)) for i in range(8)]

        def sem(name):
            return es.enter_context(nc.semaphore(name))

        S = Sched()
        for e in ("pe", "act", "dve", "sp", "pool"):
            S.add_engine(e, sem("s_" + e))

        def dsem(name):
            return DmaSem(sem("d_" + name))

        Rx = [[Res(f"x{k}_{b}") for b in range(NBLK + 1)] for k in range(KC)]
        Rbank = [Res(f"bank{i}") for i in range(8)]
        Rring = [Res(f"ring{i}") for i in range(NRING)]
        Dring = [dsem(f"ring{i}") for i in range(NRING)]
        Rwada = [Res("wada0"), Res("wada1")]
        Dwada = [dsem("wada0"), dsem("wada1")]
        Rbrow = [Res("brow0"), Res("brow1")]
        Dbrow = [dsem("brow0"), dsem("brow1")]
        Rmodrow = [Res("modrow0"), Res("modrow1")]
        Rconst = Res("const")
        Dconst = dsem("const")
        Rconstp = Res("constp")
        Dconstp = dsem("constp")
        Rln = Res("ln")
        Dln = dsem("ln")
        RmodT = [Res(f"modT{l}") for l in range(DEPTH)]
        Rgsc = [Res(f"gsc{l}") for l in range(DEPTH)]
        Rsmall = Res("small")
        Dsmall = dsem("small")
        Dwsf = dsem("wsf")
        Dwms = dsem("wms")
        Rwm = Res("wmT")
        Rwms = Res("wmT_s")
        Rpprev = Res("pprev_s")
        Dpprev = dsem("pprev")
        Rrstd = Res("rstd_tok")
        Rsrt = Res("srt_tok")
        Rdiag = [Res("diag0"), Res("diag1"), Res("diag_s")]
        Rxsq = [Res("xsq0")]
        Rh_m = [[Res(f"hm{i}_{k}") for k in range(KC)] for i in range(2)]
        Ru_m = [Res("um0"), Res("um1")]
        Rycat = Res("ycat")
        Rd_m = Res("dm")
        Rptok = [Res(f"ptok{i}") for i in range(NPT)]
        Rptok_s = Res("ptok_s")
        Rvtok = [Res(f"vtok{i}") for i in range(4)]
        Rvtok_s = Res("vtok_s")
        Rg_m = [Res("gm0"), Res("gm1")]
        Rbn = Res("bnst")
        Rmv = Res("mv")
        Rrv = Res("rv")
        Ra = [Res("a0"), Res("a1")]
        Da = [dsem("a0"), dsem("a1")]
        Rsg = [Res("sg0"), Res("sg1")]
        Dgfin = dsem("gfin")
        Rdram_out = Res("dram_out")
        Ronesf, Ronesb, Reps = Res("onesf"), Res("onesb"), Res("eps")
        out_toks = []

        bank_rr = [0]

        def alloc_bank():
            i = bank_rr[0]
            bank_rr[0] = (i + 1) % 6
            return i

        ring_rr = [0]

        def alloc_ring():
            i = ring_rr[0]
            ring_rr[0] = (i + 1) % NRING
            return i

        def mm(out, lhsT, rhs, start, stop, reads, writes, signal):
            return S.op("pe", lambda e: e.matmul(out, lhsT, rhs, start=start, stop=stop),
                        reads=reads, writes=writes, signal=signal)

        def act(out, in_, func, reads, writes, bias=None, scale=None):
            kw = {}
            if bias is not None:
                kw["bias"] = bias
            if scale is not None:
                kw["scale"] = scale
            return S.op("act", lambda e: e.activation(out=out, in_=in_, func=func, **kw),
                        reads=reads, writes=writes)

        def dve(fn, reads, writes):
            return S.op("dve", fn, reads=reads, writes=writes)

        def dma(q, out, in_, reads, writes, dsem_, nonc=False):
            if nonc:
                return S.op(q, lambda e: e.dma_start(out=out, in_=in_, allow_slow_non_contiguous=True),
                            reads=reads, writes=writes, dma=dsem_)
            return S.op(q, lambda e: e.dma_start(out=out, in_=in_), reads=reads, writes=writes, dma=dsem_)

        rr_evac = [0]

        def copy_any(out, in_, reads, writes):
            rr_evac[0] ^= 1
            if rr_evac[0]:
                return act(out, in_, AF.Identity, reads, writes)
            return dve(lambda e: e.tensor_copy(out=out, in_=in_), reads, writes)

        dma("sp", identf[:, :], ident_in[:, :], [], [Rconst], Dconst)
        for n in ("bm", "b4m"):
            dma("pool", bands[n][:, :, :], band_in[n][:, :, :], [], [Rconstp], Dconstp)
        for n in ("bp", "b4p"):
            dma("pool", bandsp[n][:, :, :], bandp_in[n][:, :, :], [], [Rconstp], Dconstp)
        dma("pool", bms[:, :, :], bms_in[:, :, :], [], [Rconstp], Dconstp)
        dma("pool", bps[:, :, :], bps_in[:, :, :], [], [Rconstp], Dconstp)
        for s in range(3):
            dma("sp", cT[:, :, s], c3[s].rearrange("(k p) -> p k", p=128), [], [Rconst], Dconst, nonc=True)
        Rconst2 = Res("const2")
        Dconst2 = dsem("const2")

        gain_loads = []
        for l in range(wdepth):
            gain_loads.append(lambda l=l: dma("sp", gT[:, 0, l, :], g_mix[l].rearrange("(k p) -> p k", p=128),
                                              [], [Rconst2], Dconst2, nonc=True))
            gain_loads.append(lambda l=l: dma("sp", gT[:, 1, l, :], g_ffn[l].rearrange("(k p) -> p k", p=128),
                                              [], [Rconst2], Dconst2, nonc=True))
            gain_loads.append(lambda l=l: dma("sp", psT[:, l, :], pool_scale[l].rearrange("(g p) -> p g", p=128),
                                              [], [Rconst2], Dconst2, nonc=True))
        dve(lambda e: e.memset(onesf[:, :], 1.0), [], [Ronesf])
        dve(lambda e: e.memset(onesb[:, :], 1.0), [], [Ronesb])
        dve(lambda e: e.memset(eps_t[:, :], EPS), [], [Reps])
        dve(lambda e: e.memset(pprev_s[:, :], 0.0), [], [Rpprev])
        dve(lambda e: e.memset(banks[6][:, 0:32], 0.0), [], [Rbank[6]])
        dve(lambda e: e.memset(wmT_s[:, :, :], 0.0), [], [Rwms])
        act(scT[:, :, :], cT[:, :, :], AF.Silu, [Rconst], [Rconst])

        wada_rr = [0]
        mod_pending = []

        def mod_flush():
            while mod_pending:
                mod_pending.pop(0)()

        def mod_piece(l, j, src=None):
            i = wada_rr[0]
            wada_rr[0] ^= 1
            if src is None:
                dma("pool", wada[i][:, :, :],
                    w_ada[l, :, j * 128:(j + 1) * 128].rearrange("(k p) c -> p k c", p=128),
                    [], [Rwada[i]], Dwada[i])
                wview, Rw = wada[i], Rwada[i]
            else:
                ri, rview = src
                wview, Rw = rview[:, :, (j % 4) * 128:(j % 4 + 1) * 128], Rring[ri]
            dma("sp", brT[i][:, 0:1], b_ada[l, j * 128:(j + 1) * 128].rearrange("(p o) -> p o", o=1),
                [], [Rbrow[i]], Dbrow[i])
            bk = alloc_bank()
            for k in range(KC):
                mm(banks[bk][:, 0:3], wview[:, k, :], scT[:, k, 0:3], k == 0, k == KC - 1,
                   [Rconst, Rw], [Rbank[bk]], k == KC - 1)
            dve(lambda e, i=i, bk=bk, l=l, j=j: e.tensor_scalar(
                out=modT[:, l, j, :], in0=banks[bk][:, 0:3], scalar1=brT[i][:, 0:1], scalar2=None, op0=ALU.add),
                [Rbank[bk], Rbrow[i]], [RmodT[l]])

        mod_queue = []

        def mod_work(n):
            for _ in range(n):
                if mod_queue:
                    l, j, wsrc = mod_queue.pop(0)
                    mod_piece(l, j, wsrc)
                    if j == 23:
                        mod_flush()
                        mod_finish(l, 0)
                    if j == 47:
                        mod_flush()
                        mod_finish(l, 1)

        def mod_finish(l, which):
            base = 8 if which == 0 else 32
            for s in range(3):
                dve(lambda e, s=s: e.scalar_tensor_tensor(
                    out=gsc[:, l, which, :, s], in0=modT[:, l, base:base + KC, s], scalar=1.0,
                    in1=gT[:, which, l, :], op0=ALU.add, op1=ALU.mult),
                    [RmodT[l], Rconst2], [Rgsc[l]])

        def load_piece(src_ap, a, b):
            i = alloc_ring()
            dst = ring[i][:, 0:a * b].rearrange("p (a b) -> p a b", a=a)
            dma("pool", dst, src_ap, [], [Rring[i]], Dring[i])
            return i, dst

        def load_cols(w, l, c0, ncol):
            return load_piece(w[l, :, c0:c0 + ncol].rearrange("(k p) c -> p k c", p=128), KC, ncol)

        def load_rows(w, l, r0, nch):
            return load_piece(w[l, r0:r0 + nch * 128, :].rearrange("(c p) n -> p c n", p=128), nch, D)

        def load_small(l):
            dma("pool", wpool_sb[:, :, :], w_pool[l].rearrange("g c d -> c g d"), [], [Rsmall], Dsmall)
            dma("pool", rows[0:1, 0:512], b_s[l:l + 1].rearrange("o h t -> o (h t)"), [], [Rsmall], Dsmall)
            for j in range(2):
                dma("pool", rows[0:1, 512:640].rearrange("o (h t) -> o h t", h=4)[:, :, 16 * j:16 * j + 16],
                    b_s[l:l + 1, :, 0:16], [], [Rsmall], Dsmall, nonc=True)
            dma("sp", lng_bc[:, :], ln_g[l].partition_broadcast(128), [], [Rln], Dln)
            dma("sp", lnb_bc[:, :], ln_b[l].partition_broadcast(128), [], [Rln], Dln)
            for j in range(2):
                dma("pool", pprev_s[16 * j:16 * j + 15, :], cache[l, j], [], [Rpprev], Dpprev)

        def prep_wm(l):
            dma("sp", wsf[:, :, :], w_s[l].rearrange("h t s -> t h s"), [], [Rg_m[1]], Dwsf)
            bk = alloc_bank()
            for h in range(4):
                S.op("pe", lambda e, h=h, bk=bk: e.transpose(out=banks[bk][:, h * 128:(h + 1) * 128],
                                                             in_=wsf[:, h, :], identity=identf[:, :]),
                     reads=[Rg_m[1], Rconst], writes=[Rbank[bk]], signal=(h == 3))
            dve(lambda e, bk=bk: e.tensor_copy(out=wmT[:, :, :],
                                                in_=banks[bk][:, :].rearrange("p (h t) -> p h t", h=4)),
                [Rbank[bk]], [Rwm])
            dve(lambda e: e.memset(wmT[64:128, :, 0:64], 0.0), [], [Rwm])
            for j in range(2):
                for h in range(4):
                    dma("pool", wmT_s[16 * j:16 * j + 16, h, 16 * j:16 * j + 16],
                        w_s[l, h, 0:16, 0:16].rearrange("t s -> s t"), [], [Rwms], Dwms, nonc=True)

        SBK = 6

        def stats_block(b, xq, Rxq):
            c0, nb = bcols(b)
            act(xq[:, :, 0:nb], x_sb[:, :, c0:c0 + nb], AF.Square,
                [Rx[k][b] for k in range(KC)], [Rxq])
            for k in range(KC):
                mm(banks[SBK][0:nb, b:b + 1], xq[:, k, 0:nb], onesb[:, 0:1], k == 0, k == KC - 1,
                   [Rxq, Ronesb], [Rbank[SBK]], k == KC - 1)

        def stats_finish(c0=0, c1=NBLK + 1):
            act(srt_tok[:, c0:c1], banks[SBK][:, c0:c1], AF.Sqrt, [Rbank[SBK], Reps], [Rsrt],
                bias=eps_t[:, 0:1], scale=1.0 / D)
            dve(lambda e: e.reciprocal(out=rstd_tok[:, c0:c1], in_=srt_tok[:, c0:c1]),
                [Rsrt], [Rrstd])

        sq_rr = [0]

        def tile_stats(tile):
            for b in tile:
                i = sq_rr[0]
                sq_rr[0] ^= 1
                stats_block(b, wada[i], Rwada[i])

        def stats_sq(b):
            i = sq_rr[0]
            sq_rr[0] ^= 1
            c0, nb = bcols(b)
            act(wada[i][:, :, 0:nb], x_sb[:, :, c0:c0 + nb], AF.Square,
                [Rx[k][b] for k in range(KC)], [Rwada[i]])

            def pe_part():
                for k in range(KC):
                    mm(banks[SBK][0:nb, b:b + 1], wada[i][:, k, 0:nb], onesb[:, 0:1], k == 0, k == KC - 1,
                       [Rwada[i], Ronesb], [Rbank[SBK]], k == KC - 1)
            return pe_part

        def stats(blocks):
            tile_stats(blocks)
            stats_finish()

        diag_rr = [0]

        def rstd_diag(tile):
            parts = []
            for b in tile:
                c0, nb = bcols(b)
                if b == SBLK:
                    i = 2
                    dhi = diagb_s[0:nb, 0, 0:nb]
                    dlo = diagb_s[0:nb, 1, 0:nb]
                else:
                    i = diag_rr[0]
                    diag_rr[0] ^= 1
                    dhi = diagb[i][0:nb, 0, 0:nb]
                    dlo = diagb[i][0:nb, 1, 0:nb]
                dve(lambda e, dhi=dhi, b=b, nb=nb: e.tensor_scalar(
                    out=dhi, in0=identf[0:nb, 0:nb], scalar1=rstd_tok[0:nb, b:b + 1],
                    scalar2=None, op0=ALU.mult), [Rrstd, Rconst], [Rdiag[i]])
                dve(lambda e, dhi=dhi, dlo=dlo, b=b, nb=nb: e.scalar_tensor_tensor(
                    out=dlo, in0=identf[0:nb, 0:nb], scalar=rstd_tok[0:nb, b:b + 1], in1=dhi,
                    op0=ALU.mult, op1=ALU.subtract), [Rrstd, Rconst, Rdiag[i]], [Rdiag[i]])
                parts.append((i, nb, dhi, dlo))
            return parts

        def rstd_mm(parts):
            bk = 7
            o = 0
            for (i, nb, dhi, dlo) in parts:
                mm(banks[bk][:, o:o + nb], onesb[0:nb, :], dhi, True, False,
                   [Rdiag[i], Ronesb], [Rbank[bk]], False)
                mm(banks[bk][:, o:o + nb], onesb[0:nb, :], dlo, False, True,
                   [Rdiag[i], Ronesb], [Rbank[bk]], True)
                o += nb
            return bk

        def rstd_bcast(tile):
            bk = 7
            o = 0
            for j in range(0, len(tile), 2):
                parts = rstd_diag(tile[j:j + 2])
                for (i, nb, dhi, dlo) in parts:
                    mm(banks[bk][:, o:o + nb], onesb[0:nb, :], dhi, True, False,
                       [Rdiag[i], Ronesb], [Rbank[bk]], False)
                    mm(banks[bk][:, o:o + nb], onesb[0:nb, :], dlo, False, True,
                       [Rdiag[i], Ronesb], [Rbank[bk]], True)
                    o += nb
            return bk

        def segs(tile):
            out = []
            npr = sum(BT for b in tile if b != SBLK)
            if npr:
                out.append((0, npr, 0))
            if SBLK in tile:
                out.append((npr, 16, 1))
                out.append((npr + 16, 16, 2))
            return out

        tmp_rr = [0]

        def make_h_slices(l, which, tile, hdst, Rh):
            t0 = bcols(tile[0])[0]
            bk = rstd_bcast(tile)
            shbase = 0 if which == 0 else 24

            def mk(k):
                def f():
                    for (o, n, s) in segs(tile):
                        for c0 in range(o, o + n, 256):
                            c1 = min(c0 + 256, o + n)
                            i = tmp_rr[0]
                            tmp_rr[0] ^= 1
                            dve(lambda e, i=i, c0=c0, c1=c1: e.tensor_tensor(
                                out=tmp_m[i][:, 0:c1 - c0], in0=x_sb[:, k, t0 + c0:t0 + c1], in1=banks[bk][:, c0:c1],
                                op=ALU.mult),
                                [Rx[k][b] for b in tile] + [Rbank[bk]], [Rsg[i]])
                            act(hdst[:, k, c0:c1], tmp_m[i][:, 0:c1 - c0], AF.Identity,
                                [Rsg[i], Rgsc[l], RmodT[l]], [Rh[k]],
                                bias=modT[:, l, shbase + k, s:s + 1], scale=gsc[:, l, which, k, s:s + 1])
                return f
            return [mk(k) for k in range(KC)]

        def make_h(l, which, tile, hdst, Rh, bk_pre=None):
            t0 = bcols(tile[0])[0]
            N = sum(bcols(b)[1] for b in tile)
            bk = rstd_bcast(tile) if bk_pre is None else bk_pre
            shbase = 0 if which == 0 else 24
            for k in range(KC):
                for (o, n, s) in segs(tile):
                    for c0 in range(o, o + n, 256):
                        c1 = min(c0 + 256, o + n)
                        i = tmp_rr[0]
                        tmp_rr[0] ^= 1
                        dve(lambda e, i=i, k=k, bk=bk, c0=c0, c1=c1: e.tensor_tensor(
                            out=tmp_m[i][:, 0:c1 - c0], in0=x_sb[:, k, t0 + c0:t0 + c1], in1=banks[bk][:, c0:c1],
                            op=ALU.mult),
                            [Rx[k][b] for b in tile] + [Rbank[bk]], [Rsg[i]])
                        act(hdst[:, k, c0:c1], tmp_m[i][:, 0:c1 - c0], AF.Identity,
                            [Rsg[i], Rgsc[l], RmodT[l]], [Rh[k]],
                            bias=modT[:, l, shbase + k, s:s + 1], scale=gsc[:, l, which, k, s:s + 1])

        def load_x(between):
            sq_pend = []
            for b in list(range(NBLK)) + [SBLK]:
                c0, nb = bcols(b)
                i = b % 2
                dma("sp", stage[i][0:nb, :], xin[c0:c0 + nb, :], [], [Ra[i]], Da[i])
                for q in range(2):
                    bk = alloc_bank()
                    for j in range(4):
                        k = q * 4 + j
                        S.op("pe", lambda e, i=i, k=k, j=j, bk=bk, nb=nb: e.transpose(
                            out=banks[bk][:, j * 128:j * 128 + nb], in_=stage[i][0:nb, k * 128:(k + 1) * 128],
                            identity=identf[0:nb, 0:nb]),
                            reads=[Ra[i], Rconst], writes=[Rbank[bk]], signal=(j == 3))
                    copy_any(x_sb[:, q * 4:q * 4 + 4, c0:c0 + nb],
                             banks[bk][:, :].rearrange("p (j t) -> p j t", j=4)[:, :, 0:nb],
                             [Rbank[bk]], [Rx[k][b] for k in range(q * 4, q * 4 + 4)])
                for p in sq_pend:
                    p()
                del sq_pend[:]
                sq_pend.append(stats_sq(b))
                between(b)
            for p in sq_pend:
                p()

        ptok_rr = [0]
        vtok_rr = [0]

        def new_state(l, ti, tile):
            N = sum(bcols(b)[1] for b in tile)
            return {"tile": tile, "N": N, "hb": ti % 2, "ub": ti % 2, "ptok": {}, "vtok": {}, "l": l,
                    "t0": bcols(tile[0])[0]}

        def stage_diag(st):
            st["diag"] = rstd_diag(st["tile"])

        def stage_rbc(st):
            if "diag" not in st:
                stage_diag(st)
            st["rbc"] = rstd_mm(st["diag"])

        def stage_h(st):
            if "rbc" not in st:
                stage_rbc(st)
            make_h(st["l"], 0, st["tile"], h_m[st["hb"]], Rh_m[st["hb"]], bk_pre=st["rbc"])

        def stage_u(st, slots):
            wu, N, hb, ub = slots["u"], st["N"], st["hb"], st["ub"]
            for n in range(4):
                bk = alloc_bank()
                for k in range(KC):
                    mm(banks[bk][:, 0:N], wu[1][:, k, n * 128:(n + 1) * 128], h_m[hb][:, k, 0:N],
                       k == 0, k == KC - 1, [Rring[wu[0]], Rh_m[hb][k]], [Rbank[bk]], k == KC - 1)
                act(u_m[ub][:, n, 0:N], banks[bk][:, 0:N], AF.Gelu_apprx_tanh, [Rbank[bk]], [Ru_m[ub]])

        def stage_pv(st, slots, p_only=False):
            l, tile, hb = st["l"], st["tile"], st["hb"]
            wp, wv = slots["p"], slots["v"]
            o = 0
            gl = []
            for bi, b in enumerate(tile):
                c0, nb = bcols(b)
                bkp = alloc_bank()
                for k in range(KC):
                    mm(banks[bkp][0:nb, :], h_m[hb][:, k, o:o + nb], wp[1][:, k, :], k == 0, k == KC - 1,
                       [Rring[wp[0]], Rh_m[hb][k]], [Rbank[bkp]], k == KC - 1)
                if b == SBLK:
                    pt, Rpt = ptok_s, Rptok_s
                else:
                    pi = ptok_rr[0]
                    ptok_rr[0] = (pi + 1) % NPT
                    pt, Rpt = ptok[pi], Rptok[pi]
                st["ptok"][b] = (pt, Rpt)
                if b in (NBLK - 1, SBLK) and not p_only:
                    dve(lambda e, pt=pt, nb=nb, bkp=bkp: e.tensor_copy(out=pt[0:nb, :], in_=banks[bkp][0:nb, :]),
                        [Rbank[bkp]], [Rpt])
                    dve(lambda e, nb=nb, bkp=bkp: e.tensor_copy(out=pf32[0:nb, :], in_=banks[bkp][0:nb, :]),
                        [Rbank[bkp]], [Ra[1]])
                    if b == SBLK:
                        for j in range(2):
                            out_toks.append(dma("sp", sp_s[l, j], pf32[16 * j + 1:16 * j + 16, :], [Ra[1]],
                                                [Rdram_out], Da[1]))
                    else:
                        out_toks.append(dma("sp", sp_p[l], pf32[113:128, :], [Ra[1]], [Rdram_out], Da[1]))
                else:
                    copy_any(pt[0:nb, :], banks[bkp][0:nb, :], [Rbank[bkp]], [Rpt])
                if p_only:
                    o += nb
                    continue
                bkv = alloc_bank()
                for k in range(KC):
                    mm(banks[bkv][0:nb, :], h_m[hb][:, k, o:o + nb], wv[1][:, k, :], k == 0, k == KC - 1,
                       [Rring[wv[0]], Rh_m[hb][k]], [Rbank[bkv]], k == KC - 1)
                gi = bi % 2
                if len(gl) == 2:
                    finish_ln(st, gl)
                    gl = []
                act(g_m[gi][0:nb, :], banks[bkv][0:nb, :], AF.Gelu_apprx_tanh, [Rbank[bkv]], [Rg_m[gi]])
                dve(lambda e, gi=gi, bi=bi, nb=nb: e.bn_stats(out=bnst[0:nb, bi, :], in_=g_m[gi][0:nb, :]),
                    [Rg_m[gi]], [Rbn])
                dve(lambda e, bi=bi, nb=nb: e.bn_aggr(out=mv[0:nb, bi, :], in_=bnst[0:nb, bi, :]),
                    [Rbn], [Rmv])
                gl.append((b, bi, gi, nb))
                o += nb
            if gl:
                finish_ln(st, gl)

        def finish_ln(st, gl):
            l = st["l"]
            b_lo, b_hi = gl[0][1], gl[-1][1] + 1
            act(sv[:, b_lo:b_hi], mv[:, b_lo:b_hi, 1], AF.Sqrt, [Rmv, Reps], [Rrv],
                bias=eps_t[:, 0:1], scale=1.0)
            dve(lambda e: e.reciprocal(out=rv[:, b_lo:b_hi], in_=sv[:, b_lo:b_hi]), [Rrv], [Rrv])
            for (b, bi, gi, nb) in gl:
                dve(lambda e, gi=gi, bi=bi, nb=nb: e.scalar_tensor_tensor(
                    out=g_m[gi][0:nb, :], in0=g_m[gi][0:nb, :], scalar=mv[0:nb, bi, 0:1], in1=lng_bc[0:nb, :],
                    op0=ALU.subtract, op1=ALU.mult), [Rg_m[gi], Rmv, Rln], [Rg_m[gi]])
                if b == SBLK:
                    st["vtok"][b] = (vtok_s, Rvtok_s)
                    dve(lambda e, gi=gi, bi=bi, nb=nb: e.scalar_tensor_tensor(
                        out=vf32[0:nb, :], in0=g_m[gi][0:nb, :], scalar=rv[0:nb, bi:bi + 1], in1=lnb_bc[0:nb, :],
                        op0=ALU.mult, op1=ALU.add), [Rg_m[gi], Rrv, Rln], [Ra[1]])
                    dve(lambda e, nb=nb: e.tensor_copy(out=vtok_s[0:nb, :], in_=vf32[0:nb, :]),
                        [Ra[1]], [Rvtok_s])
                    for j in range(2):
                        out_toks.append(dma("sp", sv_s[l, j], vf32[16 * j:16 * j + 16, :], [Ra[1]],
                                            [Rdram_out], Da[1]))
                else:
                    vi = vtok_rr[0]
                    vtok_rr[0] = (vi + 1) % 4
                    st["vtok"][b] = (vtok[vi], Rvtok[vi])
                    dve(lambda e, gi=gi, bi=bi, vi=vi, nb=nb: e.scalar_tensor_tensor(
                        out=vtok[vi][0:nb, :], in0=g_m[gi][0:nb, :], scalar=rv[0:nb, bi:bi + 1],
                        in1=lnb_bc[0:nb, :], op0=ALU.mult, op1=ALU.add),
                        [Rg_m[gi], Rrv, Rln], [Rvtok[vi]])

        def stage_pool(st, prev_ptok):
            tile, ub = st["tile"], st["ub"]
            o = 0
            for b in tile:
                c0, nb = bcols(b)
                if b == SBLK:
                    ppv, Rpp, kp, npv, bmain, bprev = pprev_s, Rpprev, 32, 32, bms, bps
                else:
                    ppv, Rpp = prev_ptok[b]
                    kp, npv = 128, 16
                    bmain, bprev = (bands["b4m"], bandsp["b4p"]) if b == 4 else (bands["bm"], bandsp["bp"])
                pc, Rpc = st["ptok"][b]
                bk = alloc_bank()
                for g in range(4):
                    mm(banks[bk][:, g * 128:g * 128 + nb], pc[0:nb, g * 128:(g + 1) * 128], bmain[0:nb, g, 0:nb],
                       True, False, [Rpc, Rconstp], [Rbank[bk]], False)
                    mm(banks[bk][:, g * 128:g * 128 + npv], ppv[0:kp, g * 128:(g + 1) * 128], bprev[0:kp, g, 0:npv],
                       False, True, [Rpp, Rconstp], [Rbank[bk]], g == 3)
                copy_any(d_m[:, :, o:o + nb],
                         banks[bk][:, :].rearrange("p (g t) -> p g t", g=4)[:, :, 0:nb],
                         [Rbank[bk]], [Rd_m])
                o += nb
        def stage_gmlp(st):
            tile, ub = st["tile"], st["ub"]
            o = 0
            for b in tile:
                c0, nb = bcols(b)
                vt, Rvt = st["vtok"][b]
                bk = alloc_bank()
                for hd in range(4):
                    if b == SBLK:
                        wm_ap, Rw = wmT_s[0:nb, hd, 0:nb], Rwms
                        bs_ap = rows[0:1, 512 + hd * 32:512 + hd * 32 + nb]
                    else:
                        wm_ap, Rw = wmT[:, hd, :], Rwm
                        bs_ap = rows[0:1, hd * 128:(hd + 1) * 128]
                    mm(banks[bk][:, hd * 128:hd * 128 + nb], vt[0:nb, hd * 128:(hd + 1) * 128], wm_ap,
                       True, False, [Rvt, Rw], [Rbank[bk]], False)
                    mm(banks[bk][:, hd * 128:hd * 128 + nb], onesb[0:1, :], bs_ap, False, True,
                       [Rsmall, Ronesb], [Rbank[bk]], hd == 3)
                dve(lambda e, bk=bk, o=o, nb=nb, ub=ub: e.tensor_tensor(
                    out=ycat[:, 4:8, o:o + nb], in0=u_m[ub][:, :, o:o + nb],
                    in1=banks[bk][:, :].rearrange("p (h t) -> p h t", h=4)[:, :, 0:nb], op=ALU.mult),
                    [Rbank[bk], Ru_m[ub]], [Rycat])
                o += nb

        def stage_wpool(st):
            l, N = st["l"], st["N"]
            for g in range(4):
                bk = alloc_bank()
                mm(banks[bk][:, 0:N], wpool_sb[:, g, :], d_m[:, g, 0:N], True, True,
                   [Rsmall, Rd_m], [Rbank[bk]], True)
                act(ycat[:, g, 0:N], banks[bk][:, 0:N], AF.Identity, [Rbank[bk], Rconst2], [Rycat],
                    scale=psT[:, l, g:g + 1])

        def stage_out(st, slots):
            l, tile, N, t0 = st["l"], st["tile"], st["N"], st["t0"]
            wo = slots["o"]
            for n in range(KC):
                bk = alloc_bank()
                wsl = wo[n // 4]
                for k in range(KC):
                    mm(banks[bk][:, 0:N], wsl[1][:, k, (n % 4) * 128:(n % 4 + 1) * 128], ycat[:, k, 0:N],
                       k == 0, k == KC - 1, [Rring[wsl[0]], Rycat], [Rbank[bk]], k == KC - 1)
                for (so, sn, s) in segs(tile):
                    blks = [b for b in tile if b != SBLK] if s == 0 else [SBLK]
                    dve(lambda e, bk=bk, n=n, so=so, sn=sn, s=s: e.scalar_tensor_tensor(
                        out=x_sb[:, n, t0 + so:t0 + so + sn], in0=banks[bk][:, so:so + sn],
                        scalar=modT[:, l, 16 + n, s:s + 1], in1=x_sb[:, n, t0 + so:t0 + so + sn],
                        op0=ALU.mult, op1=ALU.add),
                        [Rbank[bk], RmodT[l]] + [Rx[n][b] for b in blks], [Rx[n][b] for b in blks])

        def mixer_phase(l, mixer_bg=lambda: None, drain_hook=lambda: None):
            full = list(range(l + 1, NBLK)) + [SBLK]
            slots = {}
            slots["p"] = load_cols(w_in, l, 0, 512)
            slots["u"] = load_cols(w_in, l, 512, 512)
            slots["v"] = load_cols(w_in, l, 1024, 512)
            slots["o"] = [load_cols(w_out, l, 0, 512), load_cols(w_out, l, 512, 512)]
            S.barrier()
            stats_finish()
            tiles = tiles_of(full, 2)
            if tiles[-1] == [SBLK]:
                tiles = tiles[:-2] + [tiles[-2] + [SBLK]]
            st0 = new_state(l, -1, [l])
            stage_h(st0)
            stage_pv(st0, slots, p_only=True)
            prev_ptok = {l + 1: st0["ptok"][l]}
            n = len(tiles)
            sts = [new_state(l, i, tile) for i, tile in enumerate(tiles)]
            stage_h(sts[0])
            prep_wm(l)
            for i in range(n + 2):
                if i == n:
                    drain_hook()
                if i + 1 < n:
                    stage_diag(sts[i + 1])
                if i < n:
                    stage_u(sts[i], slots)
                if 0 <= i - 2 < n:
                    stage_out(sts[i - 2], slots)
                if 0 <= i - 3 < n:
                    tile_stats(sts[i - 3]["tile"])
                if i + 1 < n:
                    stage_rbc(sts[i + 1])
                if 0 <= i - 1 < n:
                    stage_pool(sts[i - 1], prev_ptok)
                    stage_gmlp(sts[i - 1])
                if i + 1 < n:
                    stage_h(sts[i + 1])
                if i < n:
                    stage_pv(sts[i], slots)
                    for b in sts[i]["tile"]:
                        if b != SBLK and b + 1 < NBLK:
                            prev_ptok[b + 1] = sts[i]["ptok"][b]
                if 0 <= i - 1 < n:
                    stage_wpool(sts[i - 1])
                mixer_bg()
            tile_stats(sts[n - 1]["tile"])

        def ffn_begin(l):
            full = list(range(l + 1, NBLK)) + [SBLK]
            tiles = tiles_of(full, 4)
            ctx = {"l": l, "tiles": tiles}

            def load_ffn_piece(j):
                nch = min(4, NFC - 4 * j)
                g = load_cols(w_gate, l, j * 512, nch * 128)
                u = load_cols(w_up, l, j * 512, nch * 128)
                dn = load_rows(w_down, l, j * 512, nch)
                return (nch, g, u, dn)

            ctx["load"] = load_ffn_piece
            ctx["pieces"] = [load_ffn_piece(0)]
            Rh = [[Res(f"hall{l}_{ti}_{k}") for k in range(KC)] for ti in range(len(tiles))]
            ctx["Rh"] = Rh

            def mkh(ti):
                tile = tiles[ti]
                N = sum(bcols(b)[1] for b in tile)
                make_h(l, 1, tile, h_tile(ti, N), Rh[ti])

            ctx["mkh"] = mkh

            def mkh_slices(ti):
                tile = tiles[ti]
                N = sum(bcols(b)[1] for b in tile)
                return make_h_slices(l, 1, tile, h_tile(ti, N), Rh[ti])

            ctx["mkh_slices"] = mkh_slices
            seed(Rh[0], Rh_m[0] + Rh_m[1])
            c1 = tiles[1][-1] if len(tiles) > 1 and tiles[1][-1] != SBLK else tiles[0][-1]
            stats_finish(tiles[0][0], c1 + 1)
            mkh(0)
            return ctx

        def ffn_phase(ctx, next_layer_work):
            l, tiles, Rh, mkh, pieces = ctx["l"], ctx["tiles"], ctx["Rh"], ctx["mkh"], ctx["pieces"]
            load_ffn_piece = ctx["load"]
            npieces = (NFC + 3) // 4
            regions = [None,
                       Rh_m[0] + Rh_m[1] + [Ru_m[0], Ru_m[1], Rycat],
                       [Rycat, Rd_m, Rptok_s] + Rptok,
                       [Rptok_s, Rvtok_s, Rg_m[0]] + Rptok + Rvtok,
                       [Rvtok_s, Rg_m[0], Rg_m[1]] + Rvtok]
            for ti in range(1, len(tiles)):
                seed(Rh[ti], regions[ti])
            mkh_slices = ctx["mkh_slices"]

            for j in range(npieces):
                if j + 1 < npieces:
                    pieces.append(load_ffn_piece(j + 1))
                nch, wg, wu, wd = pieces[j]

                def GU(ti, hook=None):
                    tile = tiles[ti]
                    N = sum(bcols(b)[1] for b in tile)
                    hT = h_tile(ti, N)
                    ab = ti % 2
                    for c in range(nch):
                        for _ in range(2):
                            if hook:
                                hook.pop(0)()
                        bg = alloc_bank()
                        for k in range(KC):
                            mm(banks[bg][:, 0:N], wg[1][:, k, c * 128:(c + 1) * 128], hT[:, k, 0:N],
                               k == 0, k == KC - 1, [Rring[wg[0]], Rh[ti][k]], [Rbank[bg]], k == KC - 1)
                        bu = alloc_bank()
                        for k in range(KC):
                            mm(banks[bu][:, 0:N], wu[1][:, k, c * 128:(c + 1) * 128], hT[:, k, 0:N],
                               k == 0, k == KC - 1, [Rring[wu[0]], Rh[ti][k]], [Rbank[bu]], k == KC - 1)
                        si = c % 2
                        act(sg_sb[si][:, 0:N], banks[bg][:, 0:N], AF.Silu, [Rbank[bg]], [Rsg[si]])
                        dve(lambda e, ab=ab, c=c, si=si, bu=bu, N=N: e.tensor_tensor(
                            out=a_sb[ab][:, c, 0:N], in0=sg_sb[si][:, 0:N], in1=banks[bu][:, 0:N], op=ALU.mult),
                            [Rsg[si], Rbank[bu]], [Ra[ab]])

                def DN(ti):
                    tile = tiles[ti]
                    t0 = bcols(tile[0])[0]
                    N = sum(bcols(b)[1] for b in tile)
                    ab = ti % 2
                    for n in range(KC):
                        bk = alloc_bank()
                        for c in range(nch):
                            mm(banks[bk][:, 0:N], wd[1][:, c, n * 128:(n + 1) * 128], a_sb[ab][:, c, 0:N],
                               c == 0, c == nch - 1, [Rring[wd[0]], Ra[ab]], [Rbank[bk]], c == nch - 1)
                        for (so, sn, s) in segs(tile):
                            blks = [b for b in tile if b != SBLK] if s == 0 else [SBLK]
                            dve(lambda e, bk=bk, n=n, so=so, sn=sn, s=s, t0=t0: e.scalar_tensor_tensor(
                                out=x_sb[:, n, t0 + so:t0 + so + sn], in0=banks[bk][:, so:so + sn],
                                scalar=modT[:, l, 40 + n, s:s + 1], in1=x_sb[:, n, t0 + so:t0 + so + sn],
                                op0=ALU.mult, op1=ALU.add),
                                [Rbank[bk], RmodT[l]] + [Rx[n][b] for b in blks], [Rx[n][b] for b in blks])

                if j == 0:
                    GU(0, mkh_slices(1) if len(tiles) > 1 else None)
                    stats_finish()
                else:
                    GU(0)
                last = (j == npieces - 1)
                for ti in range(len(tiles)):
                    sblk = list(tiles[ti - 1]) if (last and ti >= 1) else []
                    pend = [stats_sq(b) for b in sblk[0:2]]
                    if ti + 1 < len(tiles):
                        if j == 0 and ti + 2 < len(tiles):
                            GU(ti + 1, mkh_slices(ti + 2))
                        else:
                            GU(ti + 1)
                    for p in pend:
                        p()
                    pend = [stats_sq(b) for b in sblk[2:4]]
                    DN(ti)
                    for p in pend:
                        p()
                    for b in sblk[4:]:
                        stats_sq(b)()
                    next_layer_work(2)
                if last:
                    tile_stats(tiles[-1])

        def final_phase():
            blocks = list(range(4, NBLK)) + [SBLK]
            S.barrier()
            stats_finish()
            gi = alloc_ring()
            gfin_bc = ring[gi][:, 0:2048].bitcast(F32)
            dma("sp", gfin_bc[:, :], g_final.partition_broadcast(128), [], [Rring[gi]], Dgfin)
            for bi, b in enumerate(blocks):
                c0, nb = bcols(b)
                i = bi % 2
                for q in range(2):
                    bk = alloc_bank()
                    for j in range(4):
                        k = q * 4 + j
                        S.op("pe", lambda e, k=k, j=j, bk=bk, nb=nb, c0=c0: e.transpose(
                            out=banks[bk][0:nb, j * 128:(j + 1) * 128], in_=x_sb[:, k, c0:c0 + nb],
                            identity=identf[:, :]),
                            reads=[Rx[k][b], Rconst], writes=[Rbank[bk]], signal=(j == 3))
                    dve(lambda e, i=i, q=q, bk=bk, nb=nb, b=b: e.scalar_tensor_tensor(
                        out=stage[i][0:nb, q * 512:(q + 1) * 512], in0=banks[bk][0:nb, :],
                        scalar=rstd_tok[0:nb, b:b + 1], in1=gfin_bc[0:nb, q * 512:(q + 1) * 512],
                        op0=ALU.mult, op1=ALU.mult), [Rbank[bk], Rrstd, Rring[gi]], [Ra[i]])
                r0 = (b - 4) * BT if b != SBLK else 16 * BT
                out_toks.append(dma("sp", y_out[r0:r0 + nb, :], stage[i][0:nb, :], [Ra[i]], [Rdram_out], Da[i]))

        for g in range(6):
            ri, rview = load_cols(w_ada, 0, g * 512, 512)
            mod_queue.extend((0, 4 * g + jj, (ri, rview)) for jj in range(4))
        mod_queue.extend((0, j, None) for j in range(24, 48))
        def ring_mods(b):
            if b >= 2 and gain_loads:
                gain_loads.pop(0)()
            if b >= 8:
                for _ in range(2):
                    if mod_queue and mod_queue[0][2] is not None:
                        mod_work(1)
        load_x(ring_mods)
        while gain_loads:
            gain_loads.pop(0)()
        while mod_queue and mod_queue[0][1] < 24:
            mod_work(1)
        load_small(0)
        for l in range(n_layers):
            holder = []
            mixer_phase(l, lambda: mod_work(3), lambda l=l, holder=holder: holder.append(ffn_begin(l)))
            mod_work(48)
            if l + 1 < n_layers:
                mod_queue.extend((l + 1, j, None) for j in range(48))
            first = [True]

            def nlw(n, l=l, first=first):
                if l + 1 < n_layers:
                    if first[0]:
                        load_small(l + 1)
                        first[0] = False
                    mod_work(n)
            ffn_phase(holder[0], nlw)
            mod_work(48)
        if do_final:
            final_phase()
        fin = {}
        for t in out_toks:
            if id(t.sem) not in fin or fin[id(t.sem)][1] < t.val:
                fin[id(t.sem)] = (t.sem, t.val)
        S.eng["sp"].ops.append((list(fin.values()), None, None))

        with nc.Block() as block:
            @block.tensor
            def _(e):
                S.replay("pe", e)

            @block.scalar
            def _(e):
                S.replay("act", e)

            @block.vector
            def _(e):
                S.replay("dve", e)

            @block.gpsimd
            def _(e):
                S.replay("pool", e)

            @block.sync
            def _(e):
                S.replay("sp", e)
    return nc


def _band_consts():
    s = np.arange(128)[:, None]
    t = np.arange(128)[None, :]
    bm = np.zeros((128, 4, 128), np.float32)
    bp = np.zeros((128, 4, 128), np.float32)
    b4m = np.zeros((128, 4, 128), np.float32)
    for g, w in enumerate(POOL_W):
        inwin = ((t - s) >= 0) & ((t - s) < w)
        bm[:, g, :] = inwin / float(w) - (s == t)
        bp[:, g, :] = (((t + 128 - s) >= 0) & ((t + 128 - s) < w)) / float(w)
        cnt = np.minimum(t + 1, w).astype(np.float32)
        b4m[:, g, :] = inwin / cnt - (s == t)
    bp = np.ascontiguousarray(bp[:, :, 0:16])
    b4p = np.zeros_like(bp)
    s = np.arange(32)[:, None]
    t = np.arange(32)[None, :]
    bms = np.zeros((32, 4, 32), np.float32)
    bps = np.zeros((32, 4, 32), np.float32)
    same = (s // 16) == (t // 16)
    r = s % 16
    tl = t % 16
    for g, w in enumerate(POOL_W):
        bms[:, g, :] = (same & ((t - s) >= 0) & ((t - s) < w)) / float(w) - (s == t)
        bps[:, g, :] = (same & (r < 15) & ((tl + 15 - r) < w)) / float(w)
    return bm, bp, b4m, b4p, bms, bps


_NC_CACHE = {}


def kernel(x_prompt, x_sample, c_prompt, c_sample, cache_pool, w_ada, b_ada, g_mix, w_in,
           w_pool, pool_scale, ln_g, ln_b, w_s, b_s, w_out, g_ffn, w_gate, w_up, w_down, g_final):
    f = lambda a: np.ascontiguousarray(np.asarray(a), dtype=np.float32)
    x_prompt, x_sample, c_prompt, c_sample, cache_pool = map(f, (x_prompt, x_sample, c_prompt, c_sample, cache_pool))
    shared = dict(w_ada=f(w_ada), b_ada=f(b_ada), g_mix=f(g_mix), w_in=f(w_in), w_pool=f(w_pool),
                  pool_scale=f(pool_scale), ln_g=f(ln_g), ln_b=f(ln_b), w_s=f(w_s), b_s=f(b_s),
                  w_out=f(w_out), g_ffn=f(g_ffn), w_gate=f(w_gate), w_up=f(w_up), w_down=f(w_down),
                  g_final=f(g_final))
    bm, bp, b4m, b4p, bms, bps = _band_consts()
    shared.update(bm=bm, bp=bp, bms=bms, bps=bps, ident=np.eye(128, dtype=np.float32))
    in_maps = []
    for c in range(8):
        b, half = c // 2, c % 2
        xin = np.zeros((TT, D), np.float32)
        if half == 0:
            xin[4 * BT:NBLK * BT] = x_prompt[b, 0:16 * BT]
        else:
            xin[0:NBLK * BT] = x_prompt[b, 12 * BT:32 * BT]
        xin[NBLK * BT:NBLK * BT + 16] = x_sample[2 * c]
        xin[NBLK * BT + 16:] = x_sample[2 * c + 1]
        m = dict(shared)
        m["xin"] = xin
        m["c3"] = np.stack([c_prompt[b], c_sample[2 * c], c_sample[2 * c + 1]])
        m["cache"] = np.ascontiguousarray(cache_pool[:, 2 * c:2 * c + 2])
        m["b4m"] = b4m if half == 0 else bm
        m["b4p"] = b4p if half == 0 else bp
        in_maps.append(m)
    if "nc" not in _NC_CACHE:
        _NC_CACHE["nc"] = build()
    res = run_bass_kernel_spmd(_NC_CACHE["nc"], in_maps, core_ids=list(range(8)))
    R = res.results
    B, SEQ = x_prompt.shape[0], x_prompt.shape[1]
    y_prompt = np.zeros((B, SEQ, D), np.float32)
    y_sample = np.zeros((16, 16, D), np.float32)
    spp = np.zeros((DEPTH, B, 15, 512), np.float32)
    sps = np.zeros((DEPTH, 16, 15, 512), np.float32)
    svs = np.zeros((DEPTH, 16, 16, 512), np.float32)
    for c in range(8):
        b, half = c // 2, c % 2
        yo = R[c]["y_out"]
        y_prompt[b, half * 2048:(half + 1) * 2048] = yo[0:2048]
        y_sample[2 * c] = yo[2048:2064]
        y_sample[2 * c + 1] = yo[2064:2080]
        if half == 1:
            spp[:, b] = R[c]["sp_p"]
        sps[:, 2 * c:2 * c + 2] = R[c]["sp_s"]
        svs[:, 2 * c:2 * c + 2] = R[c]["sv_s"]
    return (y_prompt, y_sample, spp, sps, svs)
```

```python
import numpy as np
from contextlib import ExitStack
import concourse.bass as bass
import concourse.mybir as mybir
from concourse.bass_utils import run_bass_kernel_spmd

F32, BF16 = mybir.dt.float32, mybir.dt.bfloat16
AF = mybir.ActivationFunctionType
ALU = mybir.AluOpType

D = 1024
KC = 8
DEPTH = 4
DFF = 2816
NFC = 22
NBLK = 20
BT = 128
NSMP = 32
SBLK = NBLK
TT = NBLK * BT + NSMP
HT = (NBLK - 1) * BT + NSMP
EPS = 1e-6
POOL_W = (2, 4, 8, 16)
NRING = 6


class Tok:
    __slots__ = ("sem", "val")

    def __init__(self, sem, val):
        self.sem, self.val = sem, val


class Res:
    __slots__ = ("name", "w", "r")

    def __init__(self, name):
        self.name, self.w, self.r = name, None, []


class DmaSem:
    def __init__(self, sem):
        self.sem, self.count = sem, 0


class EngS:
    def __init__(self, name, sem):
        self.name, self.sem = name, sem
        self.count = 0
        self.ops = []
        self.known = {}
        self.pending = []


class Sched:
    def __init__(self):
        self.eng = {}

    def add_engine(self, name, sem):
        self.eng[name] = EngS(name, sem)

    def op(self, eng, emit, reads=(), writes=(), signal=True, dma=None):
        E = self.eng[eng]
        dsem_obj = dma.sem if dma is not None else None
        deps = []
        for r in reads:
            if r.w is not None:
                deps.append(r.w)
        for w in writes:
            if w.w is not None:
                deps.append(w.w)
            deps.extend(w.r)
        waits = {}
        for t in deps:
            if eng == "pe" and t.sem is E.sem:
                continue
            assert t.val is not None, "dependency on an unsignaled op"
            if dsem_obj is not None and t.sem is dsem_obj:
                continue
            if E.known.get(id(t.sem), 0) >= t.val:
                continue
            k = id(t.sem)
            if k not in waits or waits[k][1] < t.val:
                waits[k] = (t.sem, t.val)
        for k, (s, v) in waits.items():
            E.known[k] = v
        if dma is not None:
            dma.count += 16
            tok = Tok(dma.sem, dma.count)
            inc = (dma.sem, 16)
        elif signal:
            E.count += 1
            tok = Tok(E.sem, E.count)
            inc = (E.sem, 1)
            for p in E.pending:
                p.val = E.count
            E.pending = []
        else:
            tok = Tok(E.sem, None)
            E.pending.append(tok)
            inc = None
        E.ops.append((list(waits.values()), emit, inc))
        for r in reads:
            r.r.append(tok)
        for w in writes:
            w.w = tok
            w.r = []
        return tok

    def barrier(self):
        snap = {n: (E.sem, E.count) for n, E in self.eng.items()}
        for n, E in self.eng.items():
            assert not E.pending
            waits = []
            for m, (s, v) in snap.items():
                if m != n and v > 0 and E.known.get(id(s), 0) < v:
                    waits.append((s, v))
                    E.known[id(s)] = v
            E.ops.append((waits, None, None))

    def replay(self, name, handle):
        E = self.eng[name]
        for waits, emit, inc in E.ops:
            for (s, v) in waits:
                handle.wait_ge(s, v)
            if emit is None:
                continue
            inst = emit(handle)
            if inc is not None:
                inst.then_inc(inc[0], inc[1])


def tiles_of(blocks, per):
    full = [b for b in blocks if b != SBLK]
    groups = [full[i:i + per] for i in range(0, len(full), per)]
    if SBLK in blocks:
        if groups and len(groups[-1]) < per:
            groups[-1] = groups[-1] + [SBLK]
        else:
            groups.append([SBLK])
    return groups


def bcols(b):
    if b == SBLK:
        return NBLK * BT, NSMP
    return b * BT, BT


def build(n_layers=DEPTH, do_final=True, wdepth=DEPTH):
    nc = bass.Bass("TRN2", target_bir_lowering=False)

    def din(name, shape):
        return nc.dram_tensor(name, list(shape), F32, kind="ExternalInput").ap()

    def dout(name, shape):
        return nc.dram_tensor(name, list(shape), F32, kind="ExternalOutput").ap()

    xin = din("xin", [TT, D])
    c3 = din("c3", [3, D])
    cache = din("cache", [wdepth, 2, 15, 512])
    band_in = {n: din(n, [128, 4, 128]) for n in ("bm", "b4m")}
    bandp_in = {n: din(n, [128, 4, 16]) for n in ("bp", "b4p")}
    bms_in = din("bms", [32, 4, 32])
    bps_in = din("bps", [32, 4, 32])
    ident_in = din("ident", [128, 128])
    w_ada = din("w_ada", [wdepth, D, 6 * D])
    b_ada = din("b_ada", [wdepth, 6 * D])
    g_mix = din("g_mix", [wdepth, D])
    w_in = din("w_in", [wdepth, D, 1536])
    w_pool = din("w_pool", [wdepth, 4, 128, 128])
    pool_scale = din("pool_scale", [wdepth, 512])
    ln_g = din("ln_g", [wdepth, 512])
    ln_b = din("ln_b", [wdepth, 512])
    w_s = din("w_s", [wdepth, 4, 128, 128])
    b_s = din("b_s", [wdepth, 4, 128])
    w_out = din("w_out", [wdepth, D, D])
    g_ffn = din("g_ffn", [wdepth, D])
    w_gate = din("w_gate", [wdepth, D, DFF])
    w_up = din("w_up", [wdepth, D, DFF])
    w_down = din("w_down", [wdepth, DFF, D])
    g_final = din("g_final", [D])

    y_out = dout("y_out", [16 * BT + NSMP, D])
    sp_p = dout("sp_p", [DEPTH, 15, 512])
    sp_s = dout("sp_s", [DEPTH, 2, 15, 512])
    sv_s = dout("sv_s", [DEPTH, 2, 16, 512])

    es = ExitStack()
    with es:
        def sb(name, shape, dt):
            return es.enter_context(nc.sbuf_tensor(name, list(shape), dt))

        x_sb = sb("x_sb", [128, KC, TT], F32)
        ring = [sb(f"ring{i}", [128, 4096], BF16) for i in range(NRING)]
        hreg = sb("hreg", [128, KC * HT], BF16)
        wada = [sb(f"wada{i}", [128, KC, 128], BF16) for i in range(2)]
        b_adaT = sb("b_adaT", [128, DEPTH, 48], F32)
        a_sb = [sb(f"a{i}", [128, 4, 512], BF16) for i in range(2)]
        sg_sb = [sb(f"sg{i}", [128, 512], BF16) for i in range(2)]
        identf = sb("identf", [128, 128], F32)
        onesf = sb("onesf", [128, 128], F32)
        onesb = sb("onesb", [128, 128], BF16)
        bands = {n: sb("band_" + n, [128, 4, 128], BF16) for n in ("bm", "b4m")}
        bandsp = {n: sb("band_" + n, [128, 4, 16], BF16) for n in ("bp", "b4p")}
        bms = sb("bms_sb", [32, 4, 32], BF16)
        bps = sb("bps_sb", [32, 4, 32], BF16)
        modT = sb("modT", [128, DEPTH, 48, 3], F32)
        gsc = sb("gsc", [128, DEPTH, 2, KC, 3], F32)
        gT = sb("gT", [128, 2, DEPTH, KC], F32)
        cT = sb("cT", [128, KC, 3], F32)
        scT = sb("scT", [128, KC, 3], BF16)
        psT = sb("psT", [128, DEPTH, 4], F32)
        wpool_sb = sb("wpool", [128, 4, 128], BF16)
        wmT = sb("wmT", [128, 4, 128], BF16)
        wmT_s = sb("wmT_s", [32, 4, 32], BF16)
        rows = sb("rows", [1, 4 * 128 + 4 * 32], BF16)
        lng_bc = sb("lng_bc", [128, 512], F32)
        lnb_bc = sb("lnb_bc", [128, 512], F32)
        pprev_s = sb("pprev_s", [32, 512], BF16)
        rstd_tok = sb("rstd_tok", [128, 24], F32)
        srt_tok = sb("srt_tok", [128, 24], F32)
        bnst = sb("bnst", [128, 4, 6], F32)
        mv = sb("mv", [128, 4, 2], F32)
        rv = sb("rv", [128, 4], F32)
        sv = sb("sv", [128, 4], F32)
        diagb = [sb(f"diagb{i}", [128, 2, 128], BF16) for i in range(2)]
        diagb_s = sb("diagb_s", [32, 2, 32], BF16)
        eps_t = sb("eps_t", [128, 1], F32)

        hoff = [0]

        def hview(nelem_bf16):
            o = hoff[0]
            hoff[0] += nelem_bf16
            assert hoff[0] <= KC * HT, "hreg overflow %d" % hoff[0]
            return hreg[:, o:o + nelem_bf16]

        NM = 288
        h_m = [hview(KC * NM).rearrange("p (k t) -> p k t", k=KC) for _ in range(2)]
        u_m = [hview(4 * NM).rearrange("p (k t) -> p k t", k=4) for _ in range(2)]
        ycat = hview(KC * NM).rearrange("p (k t) -> p k t", k=KC)
        d_m = hview(4 * NM).rearrange("p (k t) -> p k t", k=4)
        NPT = 5
        ptok = [hview(512) for _ in range(NPT)]
        ptok_s = hview(512)
        vtok = [hview(512) for _ in range(4)]
        vtok_s = hview(512)
        g_m = [hview(2 * 512).bitcast(F32) for _ in range(2)]
        def h_tile(ti, N):
            return hreg[:, ti * 4096:ti * 4096 + KC * N].rearrange("p (k t) -> p k t", k=KC)

        def seed(dst, srcs):
            toks = []
            for s in srcs:
                if s.w is not None:
                    toks.append(s.w)
                toks.extend(s.r)
            for d in dst:
                d.r.extend(toks)
        tmp_m = [sg_sb[i][:, :].bitcast(F32) for i in range(2)]
        tmp_f = [a_sb[i][:, :, :].rearrange("p a b -> p (a b)").bitcast(F32) for i in range(2)]
        stage = tmp_f
        pf32 = tmp_f[1][:, 0:512]
        vf32 = tmp_f[1][:, 512:1024]
        wsf = g_m[1].rearrange("p (h s) -> p h s", h=4)

        banks = [es.enter_context(nc.psum_tensor(f"bank{i}", [128, 512], F32)) for i in range(8)]

        def sem(name):
            return es.enter_context(nc.semaphore(name))

        S = Sched()
        for e in ("pe", "act", "dve", "sp", "pool"):
            S.add_engine(e, sem("s_" + e))

        def dsem(name):
            return DmaSem(sem("d_" + name))

        Rx = [[Res(f"x{k}_{b}") for b in range(NBLK + 1)] for k in range(KC)]
        Rbank = [Res(f"bank{i}") for i in range(8)]
        Rring = [Res(f"ring{i}") for i in range(NRING)]
        Dring = [dsem(f"ring{i}") for i in range(NRING)]
        Rwada = [Res("wada0"), Res("wada1")]
        Dwada = [dsem("wada0"), dsem("wada1")]
        Rbrow = [Res("brow0"), Res("brow1")]
        Dbrow = [dsem("brow0"), dsem("brow1")]
        Rmodrow = [Res("modrow0"), Res("modrow1")]
        Rbada = [Res(f"bada{l}") for l in range(DEPTH)]
        Dbada = dsem("bada")
        Rconst = Res("const")
        Dconst = dsem("const")
        Rconstp = Res("constp")
        Dconstp = dsem("constp")
        Rln = Res("ln")
        Dln = dsem("ln")
        RmodT = [Res(f"modT{l}") for l in range(DEPTH)]
        Rgsc = [Res(f"gsc{l}") for l in range(DEPTH)]
        Rsmall = Res("small")
        Dsmall = dsem("small")
        Dwsf = dsem("wsf")
        Dwms = dsem("wms")
        Rwm = Res("wmT")
        Rwms = Res("wmT_s")
        Rpprev = Res("pprev_s")
        Dpprev = dsem("pprev")
        Rrstd = Res("rstd_tok")
        Rsrt = Res("srt_tok")
        Rdiag = [Res("diag0"), Res("diag1"), Res("diag_s")]
        Rxsq = [Res("xsq0")]
        Rh_m = [[Res(f"hm{i}_{k}") for k in range(KC)] for i in range(2)]
        Ru_m = [Res("um0"), Res("um1")]
        Rycat = Res("ycat")
        Rd_m = Res("dm")
        Rptok = [Res(f"ptok{i}") for i in range(NPT)]
        Rptok_s = Res("ptok_s")
        Rvtok = [Res(f"vtok{i}") for i in range(4)]
        Rvtok_s = Res("vtok_s")
        Rg_m = [Res("gm0"), Res("gm1")]
        Rbn = Res("bnst")
        Rmv = Res("mv")
        Rrv = Res("rv")
        Ra = [Res("a0"), Res("a1")]
        Da = [dsem("a0"), dsem("a1")]
        Rsg = [Res("sg0"), Res("sg1")]
        Dgfin = dsem("gfin")
        Rdram_out = Res("dram_out")
        Ronesf, Ronesb, Reps = Res("onesf"), Res("onesb"), Res("eps")
        out_toks = []

        bank_rr = [0]

        def alloc_bank():
            i = bank_rr[0]
            bank_rr[0] = (i + 1) % 6
            return i

        ring_rr = [0]

        def alloc_ring():
            i = ring_rr[0]
            ring_rr[0] = (i + 1) % NRING
            return i

        def mm(out, lhsT, rhs, start, stop, reads, writes, signal):
            return S.op("pe", lambda e: e.matmul(out, lhsT, rhs, start=start, stop=stop),
                        reads=reads, writes=writes, signal=signal)

        def act(out, in_, func, reads, writes, bias=None, scale=None):
            kw = {}
            if bias is not None:
                kw["bias"] = bias
            if scale is not None:
                kw["scale"] = scale
            return S.op("act", lambda e: e.activation(out=out, in_=in_, func=func, **kw),
                        reads=reads, writes=writes)

        def dve(fn, reads, writes):
            return S.op("dve", fn, reads=reads, writes=writes)

        def dma(q, out, in_, reads, writes, dsem_, nonc=False):
            if nonc:
                return S.op(q, lambda e: e.dma_start(out=out, in_=in_, allow_slow_non_contiguous=True),
                            reads=reads, writes=writes, dma=dsem_)
            return S.op(q, lambda e: e.dma_start(out=out, in_=in_), reads=reads, writes=writes, dma=dsem_)

        rr_evac = [0]

        def copy_any(out, in_, reads, writes):
            rr_evac[0] ^= 1
            if rr_evac[0]:
                return act(out, in_, AF.Identity, reads, writes)
            return dve(lambda e: e.tensor_copy(out=out, in_=in_), reads, writes)

        dma("sp", identf[:, :], ident_in[:, :], [], [Rconst], Dconst)
        for n in ("bm", "b4m"):
            dma("pool", bands[n][:, :, :], band_in[n][:, :, :], [], [Rconstp], Dconstp)
        for n in ("bp", "b4p"):
            dma("pool", bandsp[n][:, :, :], bandp_in[n][:, :, :], [], [Rconstp], Dconstp)
        dma("pool", bms[:, :, :], bms_in[:, :, :], [], [Rconstp], Dconstp)
        dma("pool", bps[:, :, :], bps_in[:, :, :], [], [Rconstp], Dconstp)
        for s in range(3):
            dma("sp", cT[:, :, s], c3[s].rearrange("(k p) -> p k", p=128), [], [Rconst], Dconst, nonc=True)
        Rconst2 = Res("const2")
        Dconst2 = dsem("const2")

        gain_loads = []
        for l in range(wdepth):
            gain_loads.append(lambda l=l: dma("sp", gT[:, 0, l, :], g_mix[l].rearrange("(k p) -> p k", p=128),
                                              [], [Rconst2], Dconst2, nonc=True))
            gain_loads.append(lambda l=l: dma("sp", gT[:, 1, l, :], g_ffn[l].rearrange("(k p) -> p k", p=128),
                                              [], [Rconst2], Dconst2, nonc=True))
            gain_loads.append(lambda l=l: dma("sp", psT[:, l, :], pool_scale[l].rearrange("(g p) -> p g", p=128),
                                              [], [Rconst2], Dconst2, nonc=True))
        dve(lambda e: e.memset(onesf[:, :], 1.0), [], [Ronesf])
        dve(lambda e: e.memset(onesb[:, :], 1.0), [], [Ronesb])
        dve(lambda e: e.memset(eps_t[:, :], EPS), [], [Reps])
        dve(lambda e: e.memset(pprev_s[:, :], 0.0), [], [Rpprev])
        dve(lambda e: e.memset(banks[6][:, 0:32], 0.0), [], [Rbank[6]])
        dve(lambda e: e.memset(wmT_s[:, :, :], 0.0), [], [Rwms])
        act(scT[:, :, :], cT[:, :, :], AF.Silu, [Rconst], [Rconst])

        def load_bada(l):
            for q in range(6):
                dma("sp", b_adaT[:, l, q * 8:(q + 1) * 8],
                    b_ada[l, q * 1024:(q + 1) * 1024].rearrange("(j p) -> p j", p=128),
                    [], [Rbada[l]], Dbada, nonc=True)

        load_bada(0)
        wada_rr = [0]
        mod_pending = []

        def mod_flush():
            while mod_pending:
                mod_pending.pop(0)()

        def mod_piece(l, j, src=None):
            i = wada_rr[0]
            wada_rr[0] ^= 1
            if src is None:
                dma("pool", wada[i][:, :, :],
                    w_ada[l, :, j * 128:(j + 1) * 128].rearrange("(k p) c -> p k c", p=128),
                    [], [Rwada[i]], Dwada[i])
                wview, Rw = wada[i], Rwada[i]
            else:
                ri, rview = src
                wview, Rw = rview[:, :, (j % 4) * 128:(j % 4 + 1) * 128], Rring[ri]
            bk = alloc_bank()
            for k in range(KC):
                mm(banks[bk][:, 0:3], wview[:, k, :], scT[:, k, 0:3], k == 0, k == KC - 1,
                   [Rconst, Rw], [Rbank[bk]], k == KC - 1)
            dve(lambda e, bk=bk, l=l, j=j: e.tensor_scalar(
                out=modT[:, l, j, :], in0=banks[bk][:, 0:3], scalar1=b_adaT[:, l, j:j + 1], scalar2=None,
                op0=ALU.add), [Rbank[bk], Rbada[l]], [RmodT[l]])

        mod_queue = []

        def mod_work(n):
            for _ in range(n):
                if mod_queue:
                    l, j, wsrc = mod_queue.pop(0)
                    mod_piece(l, j, wsrc)
                    if j == 23:
                        mod_flush()
                        mod_finish(l, 0)
                    if j == 47:
                        mod_flush()
                        mod_finish(l, 1)

        def mod_finish(l, which):
            base = 8 if which == 0 else 32
            for s in range(3):
                dve(lambda e, s=s: e.scalar_tensor_tensor(
                    out=gsc[:, l, which, :, s], in0=modT[:, l, base:base + KC, s], scalar=1.0,
                    in1=gT[:, which, l, :], op0=ALU.add, op1=ALU.mult),
                    [RmodT[l], Rconst2], [Rgsc[l]])

        def load_piece(src_ap, a, b):
            i = alloc_ring()
            dst = ring[i][:, 0:a * b].rearrange("p (a b) -> p a b", a=a)
            dma("pool", dst, src_ap, [], [Rring[i]], Dring[i])
            return i, dst

        def load_cols(w, l, c0, ncol):
            return load_piece(w[l, :, c0:c0 + ncol].rearrange("(k p) c -> p k c", p=128), KC, ncol)

        def load_rows(w, l, r0, nch):
            return load_piece(w[l, r0:r0 + nch * 128, :].rearrange("(c p) n -> p c n", p=128), nch, D)

        def load_small(l):
            dma("pool", wpool_sb[:, :, :], w_pool[l].rearrange("g c d -> c g d"), [], [Rsmall], Dsmall)
            dma("pool", rows[0:1, 0:512], b_s[l:l + 1].rearrange("o h t -> o (h t)"), [], [Rsmall], Dsmall)
            for j in range(2):
                dma("pool", rows[0:1, 512:640].rearrange("o (h t) -> o h t", h=4)[:, :, 16 * j:16 * j + 16],
                    b_s[l:l + 1, :, 0:16], [], [Rsmall], Dsmall, nonc=True)
            dma("sp", lng_bc[:, :], ln_g[l].partition_broadcast(128), [], [Rln], Dln)
            dma("sp", lnb_bc[:, :], ln_b[l].partition_broadcast(128), [], [Rln], Dln)
            for j in range(2):
                dma("pool", pprev_s[16 * j:16 * j + 15, :], cache[l, j], [], [Rpprev], Dpprev)

        def prep_wm(l):
            dma("sp", wsf[:, :, :], w_s[l].rearrange("h t s -> t h s"), [], [Rg_m[1]], Dwsf)
            bk = alloc_bank()
            for h in range(4):
                S.op("pe", lambda e, h=h, bk=bk: e.transpose(out=banks[bk][:, h * 128:(h + 1) * 128],
                                                             in_=wsf[:, h, :], identity=identf[:, :]),
                     reads=[Rg_m[1], Rconst], writes=[Rbank[bk]], signal=(h == 3))
            dve(lambda e, bk=bk: e.tensor_copy(out=wmT[:, :, :],
                                                in_=banks[bk][:, :].rearrange("p (h t) -> p h t", h=4)),
                [Rbank[bk]], [Rwm])
            dve(lambda e: e.memset(wmT[64:128, :, 0:64], 0.0), [], [Rwm])
            for j in range(2):
                for h in range(4):
                    dma("pool", wmT_s[16 * j:16 * j + 16, h, 16 * j:16 * j + 16],
                        w_s[l, h, 0:16, 0:16].rearrange("t s -> s t"), [], [Rwms], Dwms, nonc=True)

        SBK = 6

        def stats_block(b, xq, Rxq):
            c0, nb = bcols(b)
            act(xq[:, :, 0:nb], x_sb[:, :, c0:c0 + nb], AF.Square,
                [Rx[k][b] for k in range(KC)], [Rxq])
            for k in range(KC):
                mm(banks[SBK][0:nb, b:b + 1], xq[:, k, 0:nb], onesb[:, 0:1], k == 0, k == KC - 1,
                   [Rxq, Ronesb], [Rbank[SBK]], k == KC - 1)

        def stats_finish(c0=0, c1=NBLK + 1):
            act(srt_tok[:, c0:c1], banks[SBK][:, c0:c1], AF.Sqrt, [Rbank[SBK], Reps], [Rsrt],
                bias=eps_t[:, 0:1], scale=1.0 / D)
            dve(lambda e: e.reciprocal(out=rstd_tok[:, c0:c1], in_=srt_tok[:, c0:c1]),
                [Rsrt], [Rrstd])

        sq_rr = [0]

        def tile_stats(tile):
            for b in tile:
                i = sq_rr[0]
                sq_rr[0] ^= 1
                stats_block(b, wada[i], Rwada[i])

        def stats_sq(b):
            i = sq_rr[0]
            sq_rr[0] ^= 1
            c0, nb = bcols(b)
            act(wada[i][:, :, 0:nb], x_sb[:, :, c0:c0 + nb], AF.Square,
                [Rx[k][b] for k in range(KC)], [Rwada[i]])

            def pe_part():
                for k in range(KC):
                    mm(banks[SBK][0:nb, b:b + 1], wada[i][:, k, 0:nb], onesb[:, 0:1], k == 0, k == KC - 1,
                       [Rwada[i], Ronesb], [Rbank[SBK]], k == KC - 1)
            return pe_part

        def stats(blocks):
            tile_stats(blocks)
            stats_finish()

        diag_rr = [0]

        def rstd_diag(tile):
            parts = []
            for b in tile:
                c0, nb = bcols(b)
                if b == SBLK:
                    i = 2
                    dhi = diagb_s[0:nb, 0, 0:nb]
                    dlo = diagb_s[0:nb, 1, 0:nb]
                else:
                    i = diag_rr[0]
                    diag_rr[0] ^= 1
                    dhi = diagb[i][0:nb, 0, 0:nb]
                    dlo = diagb[i][0:nb, 1, 0:nb]
                dve(lambda e, dhi=dhi, b=b, nb=nb: e.tensor_scalar(
                    out=dhi, in0=identf[0:nb, 0:nb], scalar1=rstd_tok[0:nb, b:b + 1],
                    scalar2=None, op0=ALU.mult), [Rrstd, Rconst], [Rdiag[i]])
                dve(lambda e, dhi=dhi, dlo=dlo, b=b, nb=nb: e.scalar_tensor_tensor(
                    out=dlo, in0=identf[0:nb, 0:nb], scalar=rstd_tok[0:nb, b:b + 1], in1=dhi,
                    op0=ALU.mult, op1=ALU.subtract), [Rrstd, Rconst, Rdiag[i]], [Rdiag[i]])
                parts.append((i, nb, dhi, dlo))
            return parts

        def rstd_mm(parts):
            bk = 7
            o = 0
            for (i, nb, dhi, dlo) in parts:
                mm(banks[bk][:, o:o + nb], onesb[0:nb, :], dhi, True, False,
                   [Rdiag[i], Ronesb], [Rbank[bk]], False)
                mm(banks[bk][:, o:o + nb], onesb[0:nb, :], dlo, False, True,
                   [Rdiag[i], Ronesb], [Rbank[bk]], True)
                o += nb
            return bk

        def rstd_bcast(tile):
            bk = 7
            o = 0
            for j in range(0, len(tile), 2):
                parts = rstd_diag(tile[j:j + 2])
                for (i, nb, dhi, dlo) in parts:
                    mm(banks[bk][:, o:o + nb], onesb[0:nb, :], dhi, True, False,
                       [Rdiag[i], Ronesb], [Rbank[bk]], False)
                    mm(banks[bk][:, o:o + nb], onesb[0:nb, :], dlo, False, True,
                       [Rdiag[i], Ronesb], [Rbank[bk]], True)
                    o += nb
            return bk

        def segs(tile):
            out = []
            npr = sum(BT for b in tile if b != SBLK)
            if npr:
                out.append((0, npr, 0))
            if SBLK in tile:
                out.append((npr, 16, 1))
                out.append((npr + 16, 16, 2))
            return out

        tmp_rr = [0]

        def make_h_slices(l, which, tile, hdst, Rh):
            t0 = bcols(tile[0])[0]
            bk = rstd_bcast(tile)
            shbase = 0 if which == 0 else 24

            def mk(k):
                def f():
                    for (o, n, s) in segs(tile):
                        for c0 in range(o, o + n, 256):
                            c1 = min(c0 + 256, o + n)
                            i = tmp_rr[0]
                            tmp_rr[0] ^= 1
                            dve(lambda e, i=i, c0=c0, c1=c1: e.tensor_tensor(
                                out=tmp_m[i][:, 0:c1 - c0], in0=x_sb[:, k, t0 + c0:t0 + c1], in1=banks[bk][:, c0:c1],
                                op=ALU.mult),
                                [Rx[k][b] for b in tile] + [Rbank[bk]], [Rsg[i]])
                            act(hdst[:, k, c0:c1], tmp_m[i][:, 0:c1 - c0], AF.Identity,
                                [Rsg[i], Rgsc[l], RmodT[l]], [Rh[k]],
                                bias=modT[:, l, shbase + k, s:s + 1], scale=gsc[:, l, which, k, s:s + 1])
                return f
            return [mk(k) for k in range(KC)]

        def make_h(l, which, tile, hdst, Rh, bk_pre=None):
            t0 = bcols(tile[0])[0]
            N = sum(bcols(b)[1] for b in tile)
            bk = rstd_bcast(tile) if bk_pre is None else bk_pre
            shbase = 0 if which == 0 else 24
            for k in range(KC):
                for (o, n, s) in segs(tile):
                    for c0 in range(o, o + n, 256):
                        c1 = min(c0 + 256, o + n)
                        i = tmp_rr[0]
                        tmp_rr[0] ^= 1
                        dve(lambda e, i=i, k=k, bk=bk, c0=c0, c1=c1: e.tensor_tensor(
                            out=tmp_m[i][:, 0:c1 - c0], in0=x_sb[:, k, t0 + c0:t0 + c1], in1=banks[bk][:, c0:c1],
                            op=ALU.mult),
                            [Rx[k][b] for b in tile] + [Rbank[bk]], [Rsg[i]])
                        act(hdst[:, k, c0:c1], tmp_m[i][:, 0:c1 - c0], AF.Identity,
                            [Rsg[i], Rgsc[l], RmodT[l]], [Rh[k]],
                            bias=modT[:, l, shbase + k, s:s + 1], scale=gsc[:, l, which, k, s:s + 1])

        def load_x(between):
            sq_pend = []
            for b in list(range(NBLK)) + [SBLK]:
                c0, nb = bcols(b)
                i = b % 2
                dma("sp", stage[i][0:nb, :], xin[c0:c0 + nb, :], [], [Ra[i]], Da[i])
                for q in range(2):
                    bk = alloc_bank()
                    for j in range(4):
                        k = q * 4 + j
                        S.op("pe", lambda e, i=i, k=k, j=j, bk=bk, nb=nb: e.transpose(
                            out=banks[bk][:, j * 128:j * 128 + nb], in_=stage[i][0:nb, k * 128:(k + 1) * 128],
                            identity=identf[0:nb, 0:nb]),
                            reads=[Ra[i], Rconst], writes=[Rbank[bk]], signal=(j == 3))
                    copy_any(x_sb[:, q * 4:q * 4 + 4, c0:c0 + nb],
                             banks[bk][:, :].rearrange("p (j t) -> p j t", j=4)[:, :, 0:nb],
                             [Rbank[bk]], [Rx[k][b] for k in range(q * 4, q * 4 + 4)])
                for p in sq_pend:
                    p()
                del sq_pend[:]
                sq_pend.append(stats_sq(b))
                between(b)
            for p in sq_pend:
                p()

        ptok_rr = [0]
        vtok_rr = [0]

        def new_state(l, ti, tile):
            N = sum(bcols(b)[1] for b in tile)
            return {"tile": tile, "N": N, "hb": ti % 2, "ub": ti % 2, "ptok": {}, "vtok": {}, "l": l,
                    "t0": bcols(tile[0])[0]}

        def stage_diag(st):
            st["diag"] = rstd_diag(st["tile"])

        def stage_rbc(st):
            if "diag" not in st:
                stage_diag(st)
            st["rbc"] = rstd_mm(st["diag"])

        def stage_h(st):
            if "rbc" not in st:
                stage_rbc(st)
            make_h(st["l"], 0, st["tile"], h_m[st["hb"]], Rh_m[st["hb"]], bk_pre=st["rbc"])

        def stage_u(st, slots):
            wu, N, hb, ub = slots["u"], st["N"], st["hb"], st["ub"]
            for n in range(4):
                bk = alloc_bank()
                for k in range(KC):
                    mm(banks[bk][:, 0:N], wu[1][:, k, n * 128:(n + 1) * 128], h_m[hb][:, k, 0:N],
                       k == 0, k == KC - 1, [Rring[wu[0]], Rh_m[hb][k]], [Rbank[bk]], k == KC - 1)
                act(u_m[ub][:, n, 0:N], banks[bk][:, 0:N], AF.Gelu_apprx_tanh, [Rbank[bk]], [Ru_m[ub]])

        def stage_pv(st, slots, p_only=False):
            l, tile, hb = st["l"], st["tile"], st["hb"]
            wp, wv = slots["p"], slots["v"]
            o = 0
            gl = []
            for bi, b in enumerate(tile):
                c0, nb = bcols(b)
                bkp = alloc_bank()
                for k in range(KC):
                    mm(banks[bkp][0:nb, :], h_m[hb][:, k, o:o + nb], wp[1][:, k, :], k == 0, k == KC - 1,
                       [Rring[wp[0]], Rh_m[hb][k]], [Rbank[bkp]], k == KC - 1)
                if b == SBLK:
                    pt, Rpt = ptok_s, Rptok_s
                else:
                    pi = ptok_rr[0]
                    ptok_rr[0] = (pi + 1) % NPT
                    pt, Rpt = ptok[pi], Rptok[pi]
                st["ptok"][b] = (pt, Rpt)
                if b in (NBLK - 1, SBLK) and not p_only:
                    dve(lambda e, pt=pt, nb=nb, bkp=bkp: e.tensor_copy(out=pt[0:nb, :], in_=banks[bkp][0:nb, :]),
                        [Rbank[bkp]], [Rpt])
                    dve(lambda e, nb=nb, bkp=bkp: e.tensor_copy(out=pf32[0:nb, :], in_=banks[bkp][0:nb, :]),
                        [Rbank[bkp]], [Ra[1]])
                    if b == SBLK:
                        for j in range(2):
                            out_toks.append(dma("sp", sp_s[l, j], pf32[16 * j + 1:16 * j + 16, :], [Ra[1]],
                                                [Rdram_out], Da[1]))
                    else:
                        out_toks.append(dma("sp", sp_p[l], pf32[113:128, :], [Ra[1]], [Rdram_out], Da[1]))
                else:
                    copy_any(pt[0:nb, :], banks[bkp][0:nb, :], [Rbank[bkp]], [Rpt])
                if p_only:
                    o += nb
                    continue
                bkv = alloc_bank()
                for k in range(KC):
                    mm(banks[bkv][0:nb, :], h_m[hb][:, k, o:o + nb], wv[1][:, k, :], k == 0, k == KC - 1,
                       [Rring[wv[0]], Rh_m[hb][k]], [Rbank[bkv]], k == KC - 1)
                gi = bi % 2
                if len(gl) == 2:
                    finish_ln(st, gl)
                    gl = []
                act(g_m[gi][0:nb, :], banks[bkv][0:nb, :], AF.Gelu_apprx_tanh, [Rbank[bkv]], [Rg_m[gi]])
                dve(lambda e, gi=gi, bi=bi, nb=nb: e.bn_stats(out=bnst[0:nb, bi, :], in_=g_m[gi][0:nb, :]),
                    [Rg_m[gi]], [Rbn])
                dve(lambda e, bi=bi, nb=nb: e.bn_aggr(out=mv[0:nb, bi, :], in_=bnst[0:nb, bi, :]),
                    [Rbn], [Rmv])
                gl.append((b, bi, gi, nb))
                o += nb
            if gl:
                finish_ln(st, gl)

        def finish_ln(st, gl):
            l = st["l"]
            b_lo, b_hi = gl[0][1], gl[-1][1] + 1
            act(sv[:, b_lo:b_hi], mv[:, b_lo:b_hi, 1], AF.Sqrt, [Rmv, Reps], [Rrv],
                bias=eps_t[:, 0:1], scale=1.0)
            dve(lambda e: e.reciprocal(out=rv[:, b_lo:b_hi], in_=sv[:, b_lo:b_hi]), [Rrv], [Rrv])
            for (b, bi, gi, nb) in gl:
                dve(lambda e, gi=gi, bi=bi, nb=nb: e.scalar_tensor_tensor(
                    out=g_m[gi][0:nb, :], in0=g_m[gi][0:nb, :], scalar=mv[0:nb, bi, 0:1], in1=lng_bc[0:nb, :],
                    op0=ALU.subtract, op1=ALU.mult), [Rg_m[gi], Rmv, Rln], [Rg_m[gi]])
                if b == SBLK:
                    st["vtok"][b] = (vtok_s, Rvtok_s)
                    dve(lambda e, gi=gi, bi=bi, nb=nb: e.scalar_tensor_tensor(
                        out=vf32[0:nb, :], in0=g_m[gi][0:nb, :], scalar=rv[0:nb, bi:bi + 1], in1=lnb_bc[0:nb, :],
                        op0=ALU.mult, op1=ALU.add), [Rg_m[gi], Rrv, Rln], [Ra[1]])
                    dve(lambda e, nb=nb: e.tensor_copy(out=vtok_s[0:nb, :], in_=vf32[0:nb, :]),
                        [Ra[1]], [Rvtok_s])
                    for j in range(2):
                        out_toks.append(dma("sp", sv_s[l, j], vf32[16 * j:16 * j + 16, :], [Ra[1]],
                                            [Rdram_out], Da[1]))
                else:
                    vi = vtok_rr[0]
                    vtok_rr[0] = (vi + 1) % 4
                    st["vtok"][b] = (vtok[vi], Rvtok[vi])
                    dve(lambda e, gi=gi, bi=bi, vi=vi, nb=nb: e.scalar_tensor_tensor(
                        out=vtok[vi][0:nb, :], in0=g_m[gi][0:nb, :], scalar=rv[0:nb, bi:bi + 1],
                        in1=lnb_bc[0:nb, :], op0=ALU.mult, op1=ALU.add),
                        [Rg_m[gi], Rrv, Rln], [Rvtok[vi]])

        def stage_pool(st, prev_ptok):
            tile, ub = st["tile"], st["ub"]
            o = 0
            for b in tile:
                c0, nb = bcols(b)
                if b == SBLK:
                    ppv, Rpp, kp, npv, bmain, bprev = pprev_s, Rpprev, 32, 32, bms, bps
                else:
                    ppv, Rpp = prev_ptok[b]
                    kp, npv = 128, 16
                    bmain, bprev = (bands["b4m"], bandsp["b4p"]) if b == 4 else (bands["bm"], bandsp["bp"])
                pc, Rpc = st["ptok"][b]
                bk = alloc_bank()
                for g in range(4):
                    mm(banks[bk][:, g * 128:g * 128 + nb], pc[0:nb, g * 128:(g + 1) * 128], bmain[0:nb, g, 0:nb],
                       True, False, [Rpc, Rconstp], [Rbank[bk]], False)
                    mm(banks[bk][:, g * 128:g * 128 + npv], ppv[0:kp, g * 128:(g + 1) * 128], bprev[0:kp, g, 0:npv],
                       False, True, [Rpp, Rconstp], [Rbank[bk]], g == 3)
                copy_any(d_m[:, :, o:o + nb],
                         banks[bk][:, :].rearrange("p (g t) -> p g t", g=4)[:, :, 0:nb],
                         [Rbank[bk]], [Rd_m])
                o += nb
        def stage_gmlp(st):
            tile, ub = st["tile"], st["ub"]
            o = 0
            for b in tile:
                c0, nb = bcols(b)
                vt, Rvt = st["vtok"][b]
                bk = alloc_bank()
                for hd in range(4):
                    if b == SBLK:
                        wm_ap, Rw = wmT_s[0:nb, hd, 0:nb], Rwms
                        bs_ap = rows[0:1, 512 + hd * 32:512 + hd * 32 + nb]
                    else:
                        wm_ap, Rw = wmT[:, hd, :], Rwm
                        bs_ap = rows[0:1, hd * 128:(hd + 1) * 128]
                    mm(banks[bk][:, hd * 128:hd * 128 + nb], vt[0:nb, hd * 128:(hd + 1) * 128], wm_ap,
                       True, False, [Rvt, Rw], [Rbank[bk]], False)
                    mm(banks[bk][:, hd * 128:hd * 128 + nb], onesb[0:1, :], bs_ap, False, True,
                       [Rsmall, Ronesb], [Rbank[bk]], hd == 3)
                dve(lambda e, bk=bk, o=o, nb=nb, ub=ub: e.tensor_tensor(
                    out=ycat[:, 4:8, o:o + nb], in0=u_m[ub][:, :, o:o + nb],
                    in1=banks[bk][:, :].rearrange("p (h t) -> p h t", h=4)[:, :, 0:nb], op=ALU.mult),
                    [Rbank[bk], Ru_m[ub]], [Rycat])
                o += nb

        def stage_wpool(st):
            l, N = st["l"], st["N"]
            for g in range(4):
                bk = alloc_bank()
                mm(banks[bk][:, 0:N], wpool_sb[:, g, :], d_m[:, g, 0:N], True, True,
                   [Rsmall, Rd_m], [Rbank[bk]], True)
                act(ycat[:, g, 0:N], banks[bk][:, 0:N], AF.Identity, [Rbank[bk], Rconst2], [Rycat],
                    scale=psT[:, l, g:g + 1])

        def stage_out(st, slots):
            l, tile, N, t0 = st["l"], st["tile"], st["N"], st["t0"]
            wo = slots["o"]
            for n in range(KC):
                bk = alloc_bank()
                wsl = wo[n // 4]
                for k in range(KC):
                    mm(banks[bk][:, 0:N], wsl[1][:, k, (n % 4) * 128:(n % 4 + 1) * 128], ycat[:, k, 0:N],
                       k == 0, k == KC - 1, [Rring[wsl[0]], Rycat], [Rbank[bk]], k == KC - 1)
                for (so, sn, s) in segs(tile):
                    blks = [b for b in tile if b != SBLK] if s == 0 else [SBLK]
                    dve(lambda e, bk=bk, n=n, so=so, sn=sn, s=s: e.scalar_tensor_tensor(
                        out=x_sb[:, n, t0 + so:t0 + so + sn], in0=banks[bk][:, so:so + sn],
                        scalar=modT[:, l, 16 + n, s:s + 1], in1=x_sb[:, n, t0 + so:t0 + so + sn],
                        op0=ALU.mult, op1=ALU.add),
                        [Rbank[bk], RmodT[l]] + [Rx[n][b] for b in blks], [Rx[n][b] for b in blks])

        def mixer_phase(l, mixer_bg=lambda: None, drain_hook=lambda: None):
            full = list(range(l + 1, NBLK)) + [SBLK]
            slots = {}
            slots["p"] = load_cols(w_in, l, 0, 512)
            slots["u"] = load_cols(w_in, l, 512, 512)
            slots["v"] = load_cols(w_in, l, 1024, 512)
            slots["o"] = [load_cols(w_out, l, 0, 512), load_cols(w_out, l, 512, 512)]
            S.barrier()
            stats_finish()
            tiles = tiles_of(full, 2)
            if tiles[-1] == [SBLK]:
                tiles = tiles[:-2] + [tiles[-2] + [SBLK]]
            st0 = new_state(l, -1, [l])
            stage_h(st0)
            stage_pv(st0, slots, p_only=True)
            prev_ptok = {l + 1: st0["ptok"][l]}
            n = len(tiles)
            sts = [new_state(l, i, tile) for i, tile in enumerate(tiles)]
            stage_h(sts[0])
            prep_wm(l)
            for i in range(n + 2):
                if i == n:
                    drain_hook()
                if i + 1 < n:
                    stage_diag(sts[i + 1])
                if i < n:
                    stage_u(sts[i], slots)
                if 0 <= i - 2 < n:
                    stage_out(sts[i - 2], slots)
                if 0 <= i - 3 < n:
                    tile_stats(sts[i - 3]["tile"])
                if i + 1 < n:
                    stage_rbc(sts[i + 1])
                if 0 <= i - 1 < n:
                    stage_pool(sts[i - 1], prev_ptok)
                    stage_gmlp(sts[i - 1])
                if i + 1 < n:
                    stage_h(sts[i + 1])
                if i < n:
                    stage_pv(sts[i], slots)
                    for b in sts[i]["tile"]:
                        if b != SBLK and b + 1 < NBLK:
                            prev_ptok[b + 1] = sts[i]["ptok"][b]
                if 0 <= i - 1 < n:
                    stage_wpool(sts[i - 1])
                mixer_bg()
            tile_stats(sts[n - 1]["tile"])

        def ffn_begin(l):
            full = list(range(l + 1, NBLK)) + [SBLK]
            tiles = tiles_of(full, 4)
            ctx = {"l": l, "tiles": tiles}

            def load_ffn_piece(j):
                nch = min(4, NFC - 4 * j)
                g = load_cols(w_gate, l, j * 512, nch * 128)
                u = load_cols(w_up, l, j * 512, nch * 128)
                dn = load_rows(w_down, l, j * 512, nch)
                return (nch, g, u, dn)

            ctx["load"] = load_ffn_piece
            ctx["pieces"] = [load_ffn_piece(0)]
            Rh = [[Res(f"hall{l}_{ti}_{k}") for k in range(KC)] for ti in range(len(tiles))]
            ctx["Rh"] = Rh

            def mkh(ti):
                tile = tiles[ti]
                N = sum(bcols(b)[1] for b in tile)
                make_h(l, 1, tile, h_tile(ti, N), Rh[ti])

            ctx["mkh"] = mkh

            def mkh_slices(ti):
                tile = tiles[ti]
                N = sum(bcols(b)[1] for b in tile)
                return make_h_slices(l, 1, tile, h_tile(ti, N), Rh[ti])

            ctx["mkh_slices"] = mkh_slices
            seed(Rh[0], Rh_m[0] + Rh_m[1])
            c1 = tiles[1][-1] if len(tiles) > 1 and tiles[1][-1] != SBLK else tiles[0][-1]
            stats_finish(tiles[0][0], c1 + 1)
            mkh(0)
            return ctx

        def ffn_phase(ctx, next_layer_work):
            l, tiles, Rh, mkh, pieces = ctx["l"], ctx["tiles"], ctx["Rh"], ctx["mkh"], ctx["pieces"]
            load_ffn_piece = ctx["load"]
            npieces = (NFC + 3) // 4
            regions = [None,
                       Rh_m[0] + Rh_m[1] + [Ru_m[0], Ru_m[1], Rycat],
                       [Rycat, Rd_m, Rptok_s] + Rptok,
                       [Rptok_s, Rvtok_s, Rg_m[0]] + Rptok + Rvtok,
                       [Rvtok_s, Rg_m[0], Rg_m[1]] + Rvtok]
            for ti in range(1, len(tiles)):
                seed(Rh[ti], regions[ti])
            mkh_slices = ctx["mkh_slices"]

            for j in range(npieces):
                if j + 1 < npieces:
                    pieces.append(load_ffn_piece(j + 1))
                nch, wg, wu, wd = pieces[j]

                def GU(ti, hook=None):
                    tile = tiles[ti]
                    N = sum(bcols(b)[1] for b in tile)
                    hT = h_tile(ti, N)
                    ab = ti % 2
                    for c in range(nch):
                        for _ in range(2):
                            if hook:
                                hook.pop(0)()
                        bg = alloc_bank()
                        for k in range(KC):
                            mm(banks[bg][:, 0:N], wg[1][:, k, c * 128:(c + 1) * 128], hT[:, k, 0:N],
                               k == 0, k == KC - 1, [Rring[wg[0]], Rh[ti][k]], [Rbank[bg]], k == KC - 1)
                        bu = alloc_bank()
                        for k in range(KC):
                            mm(banks[bu][:, 0:N], wu[1][:, k, c * 128:(c + 1) * 128], hT[:, k, 0:N],
                               k == 0, k == KC - 1, [Rring[wu[0]], Rh[ti][k]], [Rbank[bu]], k == KC - 1)
                        si = c % 2
                        act(sg_sb[si][:, 0:N], banks[bg][:, 0:N], AF.Silu, [Rbank[bg]], [Rsg[si]])
                        dve(lambda e, ab=ab, c=c, si=si, bu=bu, N=N: e.tensor_tensor(
                            out=a_sb[ab][:, c, 0:N], in0=sg_sb[si][:, 0:N], in1=banks[bu][:, 0:N], op=ALU.mult),
                            [Rsg[si], Rbank[bu]], [Ra[ab]])

                def DN(ti):
                    tile = tiles[ti]
                    t0 = bcols(tile[0])[0]
                    N = sum(bcols(b)[1] for b in tile)
                    ab = ti % 2
                    for n in range(KC):
                        bk = alloc_bank()
                        for c in range(nch):
                            mm(banks[bk][:, 0:N], wd[1][:, c, n * 128:(n + 1) * 128], a_sb[ab][:, c, 0:N],
                               c == 0, c == nch - 1, [Rring[wd[0]], Ra[ab]], [Rbank[bk]], c == nch - 1)
                        for (so, sn, s) in segs(tile):
                            blks = [b for b in tile if b != SBLK] if s == 0 else [SBLK]
                            dve(lambda e, bk=bk, n=n, so=so, sn=sn, s=s, t0=t0: e.scalar_tensor_tensor(
                                out=x_sb[:, n, t0 + so:t0 + so + sn], in0=banks[bk][:, so:so + sn],
                                scalar=modT[:, l, 40 + n, s:s + 1], in1=x_sb[:, n, t0 + so:t0 + so + sn],
                                op0=ALU.mult, op1=ALU.add),
                                [Rbank[bk], RmodT[l]] + [Rx[n][b] for b in blks], [Rx[n][b] for b in blks])

                if j == 0:
                    GU(0, mkh_slices(1) if len(tiles) > 1 else None)
                    stats_finish()
                else:
                    GU(0)
                last = (j == npieces - 1)
                for ti in range(len(tiles)):
                    sblk = list(tiles[ti - 1]) if (last and ti >= 1) else []
                    pend = [stats_sq(b) for b in sblk[0:2]]
                    if ti + 1 < len(tiles):
                        if j == 0 and ti + 2 < len(tiles):
                            GU(ti + 1, mkh_slices(ti + 2))
                        else:
                            GU(ti + 1)
                    for p in pend:
                        p()
                    pend = [stats_sq(b) for b in sblk[2:4]]
                    DN(ti)
                    for p in pend:
                        p()
                    for b in sblk[4:]:
                        stats_sq(b)()
                    next_layer_work(2)
                if last:
                    tile_stats(tiles[-1])

        def final_phase():
            blocks = list(range(4, NBLK)) + [SBLK]
            S.barrier()
            stats_finish()
            gi = alloc_ring()
            gfin_bc = ring[gi][:, 0:2048].bitcast(F32)
            dma("sp", gfin_bc[:, :], g_final.partition_broadcast(128), [], [Rring[gi]], Dgfin)
            for bi, b in enumerate(blocks):
                c0, nb = bcols(b)
                i = bi % 2
                for q in range(2):
                    bk = alloc_bank()
                    for j in range(4):
                        k = q * 4 + j
                        S.op("pe", lambda e, k=k, j=j, bk=bk, nb=nb, c0=c0: e.transpose(
                            out=banks[bk][0:nb, j * 128:(j + 1) * 128], in_=x_sb[:, k, c0:c0 + nb],
                            identity=identf[:, :]),
                            reads=[Rx[k][b], Rconst], writes=[Rbank[bk]], signal=(j == 3))
                    dve(lambda e, i=i, q=q, bk=bk, nb=nb, b=b: e.scalar_tensor_tensor(
                        out=stage[i][0:nb, q * 512:(q + 1) * 512], in0=banks[bk][0:nb, :],
                        scalar=rstd_tok[0:nb, b:b + 1], in1=gfin_bc[0:nb, q * 512:(q + 1) * 512],
                        op0=ALU.mult, op1=ALU.mult), [Rbank[bk], Rrstd, Rring[gi]], [Ra[i]])
                r0 = (b - 4) * BT if b != SBLK else 16 * BT
                out_toks.append(dma("sp", y_out[r0:r0 + nb, :], stage[i][0:nb, :], [Ra[i]], [Rdram_out], Da[i]))

        for g in range(6):
            ri, rview = load_cols(w_ada, 0, g * 512, 512)
            mod_queue.extend((0, 4 * g + jj, (ri, rview)) for jj in range(4))
        mod_queue.extend((0, j, None) for j in range(24, 48))
        def ring_mods(b):
            if b >= 2 and gain_loads:
                gain_loads.pop(0)()
            if b >= 8:
                for _ in range(2):
                    if mod_queue and mod_queue[0][2] is not None:
                        mod_work(1)
        load_x(ring_mods)
        while gain_loads:
            gain_loads.pop(0)()
        while mod_queue and mod_queue[0][1] < 24:
            mod_work(1)
        load_small(0)
        for l in range(n_layers):
            holder = []
            mixer_phase(l, lambda: mod_work(3), lambda l=l, holder=holder: holder.append(ffn_begin(l)))
            mod_work(48)
            if l + 1 < n_layers:
                mod_queue.extend((l + 1, j, None) for j in range(48))
            first = [True]

            def nlw(n, l=l, first=first):
                if l + 1 < n_layers:
                    if first[0]:
                        load_bada(l + 1)
                        load_small(l + 1)
                        first[0] = False
                    mod_work(n)
            ffn_phase(holder[0], nlw)
            mod_work(48)
        if do_final:
            final_phase()
        fin = {}
        for t in out_toks:
            if id(t.sem) not in fin or fin[id(t.sem)][1] < t.val:
                fin[id(t.sem)] = (t.sem, t.val)
        S.eng["sp"].ops.append((list(fin.values()), None, None))

        with nc.Block() as block:
            @block.tensor
            def _(e):
                S.replay("pe", e)

            @block.scalar
            def _(e):
                S.replay("act", e)

            @block.vector
            def _(e):
                S.replay("dve", e)

            @block.gpsimd
            def _(e):
                S.replay("pool", e)

            @block.sync
            def _(e):
                S.replay("sp", e)
    return nc


def _band_consts():
    s = np.arange(128)[:, None]
    t = np.arange(128)[None, :]
    bm = np.zeros((128, 4, 128), np.float32)
    bp = np.zeros((128, 4, 128), np.float32)
    b4m = np.zeros((128, 4, 128), np.float32)
    for g, w in enumerate(POOL_W):
        inwin = ((t - s) >= 0) & ((t - s) < w)
        bm[:, g, :] = inwin / float(w) - (s == t)
        bp[:, g, :] = (((t + 128 - s) >= 0) & ((t + 128 - s) < w)) / float(w)
        cnt = np.minimum(t + 1, w).astype(np.float32)
        b4m[:, g, :] = inwin / cnt - (s == t)
    bp = np.ascontiguousarray(bp[:, :, 0:16])
    b4p = np.zeros_like(bp)
    s = np.arange(32)[:, None]
    t = np.arange(32)[None, :]
    bms = np.zeros((32, 4, 32), np.float32)
    bps = np.zeros((32, 4, 32), np.float32)
    same = (s // 16) == (t // 16)
    r = s % 16
    tl = t % 16
    for g, w in enumerate(POOL_W):
        bms[:, g, :] = (same & ((t - s) >= 0) & ((t - s) < w)) / float(w) - (s == t)
        bps[:, g, :] = (same & (r < 15) & ((tl + 15 - r) < w)) / float(w)
    return bm, bp, b4m, b4p, bms, bps


_NC_CACHE = {}


def kernel(x_prompt, x_sample, c_prompt, c_sample, cache_pool, w_ada, b_ada, g_mix, w_in,
           w_pool, pool_scale, ln_g, ln_b, w_s, b_s, w_out, g_ffn, w_gate, w_up, w_down, g_final):
    f = lambda a: np.ascontiguousarray(np.asarray(a), dtype=np.float32)
    x_prompt, x_sample, c_prompt, c_sample, cache_pool = map(f, (x_prompt, x_sample, c_prompt, c_sample, cache_pool))
    shared = dict(w_ada=f(w_ada), b_ada=f(b_ada), g_mix=f(g_mix), w_in=f(w_in), w_pool=f(w_pool),
                  pool_scale=f(pool_scale), ln_g=f(ln_g), ln_b=f(ln_b), w_s=f(w_s), b_s=f(b_s),
                  w_out=f(w_out), g_ffn=f(g_ffn), w_gate=f(w_gate), w_up=f(w_up), w_down=f(w_down),
                  g_final=f(g_final))
    bm, bp, b4m, b4p, bms, bps = _band_consts()
    shared.update(bm=bm, bp=bp, bms=bms, bps=bps, ident=np.eye(128, dtype=np.float32))
    in_maps = []
    for c in range(8):
        b, half = c // 2, c % 2
        xin = np.zeros((TT, D), np.float32)
        if half == 0:
            xin[4 * BT:NBLK * BT] = x_prompt[b, 0:16 * BT]
        else:
            xin[0:NBLK * BT] = x_prompt[b, 12 * BT:32 * BT]
        xin[NBLK * BT:NBLK * BT + 16] = x_sample[2 * c]
        xin[NBLK * BT + 16:] = x_sample[2 * c + 1]
        m = dict(shared)
        m["xin"] = xin
        m["c3"] = np.stack([c_prompt[b], c_sample[2 * c], c_sample[2 * c + 1]])
        m["cache"] = np.ascontiguousarray(cache_pool[:, 2 * c:2 * c + 2])
        m["b4m"] = b4m if half == 0 else bm
        m["b4p"] = b4p if half == 0 else bp
        in_maps.append(m)
    if "nc" not in _NC_CACHE:
        _NC_CACHE["nc"] = build()
    res = run_bass_kernel_spmd(_NC_CACHE["nc"], in_maps, core_ids=list(range(8)))
    R = res.results
    B, SEQ = x_prompt.shape[0], x_prompt.shape[1]
    y_prompt = np.zeros((B, SEQ, D), np.float32)
    y_sample = np.zeros((16, 16, D), np.float32)
    spp = np.zeros((DEPTH, B, 15, 512), np.float32)
    sps = np.zeros((DEPTH, 16, 15, 512), np.float32)
    svs = np.zeros((DEPTH, 16, 16, 512), np.float32)
    for c in range(8):
        b, half = c // 2, c % 2
        yo = R[c]["y_out"]
        y_prompt[b, half * 2048:(half + 1) * 2048] = yo[0:2048]
        y_sample[2 * c] = yo[2048:2064]
        y_sample[2 * c + 1] = yo[2064:2080]
        if half == 1:
            spp[:, b] = R[c]["sp_p"]
        sps[:, 2 * c:2 * c + 2] = R[c]["sp_s"]
        svs[:, 2 * c:2 * c + 2] = R[c]["sv_s"]
    return (y_prompt, y_sample, spp, sps, svs)
```
